# Optimizing a Trainium2 kernel written in Bass

```python
import math
import jax
import jax.numpy as jnp
from jax import lax
import numpy as np

D_MODEL = 1024
BATCH = 8
SEQ = 2048
DEPTH = 1

GDN_HEADS = 8
GDN_HEAD_DIM = 128
GDN_WIDTH = GDN_HEADS * GDN_HEAD_DIM
GDN_CONV = 4
CHUNK = 64
CONV_WIDTH = 1024
SHORT_CONV = 3
MIX_WIDTH = GDN_WIDTH + CONV_WIDTH
EPS = 1e-6

PROJ_SPLITS = (
    3 * GDN_WIDTH,
    GDN_WIDTH,
    GDN_HEADS,
    GDN_HEADS,
    CONV_WIDTH,
    CONV_WIDTH,
    CONV_WIDTH,
    CONV_WIDTH,
)
PROJ_WIDTH = sum(PROJ_SPLITS)

kernel_name = "hybrid_gdn_shortconv_block"


def rmsnorm(x, w):
    xf = x.astype(jnp.float32)
    xf = xf * lax.rsqrt(jnp.mean(xf * xf, axis=-1, keepdims=True) + EPS)
    return (xf * w.astype(jnp.float32)).astype(x.dtype)


def l2norm(x):
    return x * lax.rsqrt(jnp.sum(x * x, axis=-1, keepdims=True) + EPS)


def causal_depthwise_conv(x, w):
    K = w.shape[0]
    L = x.shape[1]
    xp = jnp.pad(x, ((0, 0), (K - 1, 0), (0, 0)))
    return sum(xp[:, j:j + L] * w[j] for j in range(K))


def gated_delta_rule_chunked(q, k, v, g, beta):
    Bsz, L, H, DK = q.shape
    DV = v.shape[-1]
    n = L // CHUNK

    def chunks(t):
        return t.reshape(Bsz, n, CHUNK, H, -1).transpose(0, 3, 1, 2, 4)

    q, k, v = chunks(q), chunks(k), chunks(v)
    g = g.reshape(Bsz, n, CHUNK, H).transpose(0, 3, 1, 2)
    beta = beta.reshape(Bsz, n, CHUNK, H).transpose(0, 3, 1, 2)
    g = jnp.cumsum(g, axis=-1)

    causal = jnp.tril(jnp.ones((CHUNK, CHUNK), dtype=bool))
    strict = jnp.tril(jnp.ones((CHUNK, CHUNK), dtype=bool), k=-1)
    decay = jnp.exp(jnp.where(causal, g[..., :, None] - g[..., None, :], -jnp.inf))

    k_beta = k * beta[..., None]
    v_beta = v * beta[..., None]
    A = jnp.where(strict, jnp.einsum('bhnid,bhnjd->bhnij', k_beta, k) * decay, 0.0)
    eye = jnp.eye(CHUNK, dtype=q.dtype)
    rhs = jnp.concatenate([v_beta, k_beta * jnp.exp(g)[..., None]], axis=-1)
    sol = lax.linalg.triangular_solve(eye + A, rhs, left_side=True, lower=True,
                                      unit_diagonal=True)
    u = sol[..., :DV]
    w = sol[..., DV:]

    attn_intra = jnp.where(causal, jnp.einsum('bhnid,bhnjd->bhnij', q, k) * decay, 0.0)
    g_last = g[..., -1]
    k_state = k * jnp.exp(g_last[..., None] - g)[..., None]
    q_decay = q * jnp.exp(g)[..., None]

    def step(S, inp):
        qd, w_c, u_c, a_c, ks, gl = inp
        v_new = u_c - jnp.einsum('bhck,bhkv->bhcv', w_c, S)
        o = jnp.einsum('bhck,bhkv->bhcv', qd, S) + jnp.einsum('bhij,bhjv->bhiv', a_c, v_new)
        S = S * jnp.exp(gl)[..., None, None] + jnp.einsum('bhck,bhcv->bhkv', ks, v_new)
        return S, o

    xs = tuple(jnp.moveaxis(t, 2, 0) for t in (q_decay, w, u, attn_intra, k_state, g_last))
    S0 = jnp.zeros((Bsz, H, DK, DV), dtype=q.dtype)
    _, o = lax.scan(step, S0, xs)
    return o.transpose(1, 0, 3, 2, 4).reshape(Bsz, L, H, DV)


def hybrid_layer(x, norm_in_w, w_in, conv_qkv_w, A_log, dt_bias, gdn_norm_w,
                 conv_w, conv_b, w_out):
    Bsz, L, _ = x.shape
    h = rmsnorm(x, norm_in_w)
    proj = h @ w_in
    split_at = [int(i) for i in np.cumsum(PROJ_SPLITS)[:-1]]
    qkv, z_g, b_g, a_g, gate_b, gate_c, h_c, z_c = jnp.split(proj, split_at, axis=-1)

    qkv = jax.nn.silu(causal_depthwise_conv(qkv, conv_qkv_w))
    q, k, v = jnp.split(qkv, 3, axis=-1)
    shp = (Bsz, L, GDN_HEADS, GDN_HEAD_DIM)
    q = l2norm(q.reshape(shp).astype(jnp.float32)) * (GDN_HEAD_DIM ** -0.5)
    k = l2norm(k.reshape(shp).astype(jnp.float32))
    v = v.reshape(shp).astype(jnp.float32)
    beta = jax.nn.sigmoid(b_g.astype(jnp.float32))
    g = -jnp.exp(A_log.astype(jnp.float32)) * jax.nn.softplus(
        a_g.astype(jnp.float32) + dt_bias.astype(jnp.float32))
    o = gated_delta_rule_chunked(q, k, v, g, beta).astype(x.dtype)
    o = rmsnorm(o, gdn_norm_w) * jax.nn.silu(z_g.reshape(shp))
    o = o.reshape(Bsz, L, GDN_WIDTH)

    y_c = gate_b * (causal_depthwise_conv(gate_c * h_c, conv_w) + conv_b)
    y_c = y_c * jax.nn.silu(z_c)

    mix = jnp.concatenate([o, y_c], axis=-1)
    return x + mix @ w_out


def setup_inputs(seed: int = 0) -> dict:
    key = jax.random.key(seed)
    ks = jax.random.split(key, 12)
    f32 = jnp.float32
    x = jax.random.normal(ks[0], (BATCH, SEQ, D_MODEL), f32)
    norm_in_w = 1.0 + 0.02 * jax.random.normal(ks[1], (DEPTH, D_MODEL), f32)
    w_in = jax.random.normal(ks[2], (DEPTH, D_MODEL, PROJ_WIDTH), f32) * D_MODEL ** -0.5
    conv_qkv_w = jax.random.normal(ks[3], (DEPTH, GDN_CONV, 3 * GDN_WIDTH), f32) * GDN_CONV ** -0.5
    A_log = jnp.log(jax.random.uniform(ks[4], (DEPTH, GDN_HEADS), f32, minval=1.0, maxval=16.0))
    dt = jnp.exp(jax.random.uniform(ks[5], (DEPTH, GDN_HEADS), f32,
                                    minval=math.log(1e-3), maxval=math.log(1e-1)))
    dt_bias = dt + jnp.log(-jnp.expm1(-dt))
    gdn_norm_w = 1.0 + 0.02 * jax.random.normal(ks[6], (DEPTH, GDN_HEAD_DIM), f32)
    conv_w = jax.random.normal(ks[7], (DEPTH, SHORT_CONV, CONV_WIDTH), f32) * SHORT_CONV ** -0.5
    conv_b = 0.01 * jax.random.normal(ks[8], (DEPTH, CONV_WIDTH), f32)
    w_out = jax.random.normal(ks[9], (DEPTH, MIX_WIDTH, D_MODEL), f32) * MIX_WIDTH ** -0.5
    final_norm_w = 1.0 + 0.02 * jax.random.normal(ks[10], (D_MODEL,), f32)
    return {"x": x, "norm_in_w": norm_in_w, "w_in": w_in, "conv_qkv_w": conv_qkv_w,
            "A_log": A_log, "dt_bias": dt_bias, "gdn_norm_w": gdn_norm_w,
            "conv_w": conv_w, "conv_b": conv_b, "w_out": w_out,
            "final_norm_w": final_norm_w}


def reference(x, norm_in_w, w_in, conv_qkv_w, A_log, dt_bias, gdn_norm_w,
              conv_w, conv_b, w_out, final_norm_w):
    for layer in range(DEPTH):
        x = hybrid_layer(x, norm_in_w[layer], w_in[layer], conv_qkv_w[layer],
                         A_log[layer], dt_bias[layer], gdn_norm_w[layer],
                         conv_w[layer], conv_b[layer], w_out[layer])
    return rmsnorm(x, final_norm_w)
```

```python
import types
import numpy as np
from contextlib import ExitStack
import concourse.bass as bass
import concourse.mybir as mybir
from concourse.bass_utils import run_bass_kernel_spmd

F32 = mybir.dt.float32
BF16 = mybir.dt.bfloat16
F32R = mybir.dt.float32r
AF = mybir.ActivationFunctionType
ALU = mybir.AluOpType

L = 2048
D = 1024
NT = 16
EPS = 1e-6
NEG = -30000.0


class Buf:
    __slots__ = ("name", "w", "r", "excl", "wset")

    def __init__(self, name, excl=False):
        self.name = name
        self.w = None
        self.wset = {}
        self.r = {}
        self.excl = excl


def _freeze(fn):
    if fn.__closure__ is None:
        return fn
    cells = []
    for c in fn.__closure__:
        try:
            cells.append(types.CellType(c.cell_contents))
        except ValueError:
            cells.append(c)
    return types.FunctionType(fn.__code__, fn.__globals__, fn.__name__, fn.__defaults__, tuple(cells))


def bufs(name, n):
    return [Buf(f"{name}{i}") for i in range(n)]


class Eng:
    def __init__(self, name, sem):
        self.name = name
        self.sem = sem
        self.count = 0
        self.known = {}
        self.prog = []
        self.hist = [None]


class FW:
    def __init__(self, nc, sems):
        self.nc = nc
        self.E = {k: Eng(k, sems[k]) for k in ("pe", "act", "dve", "pool", "sp")}
        self.dma_sems = list(sems["dma"])
        self.dma_sem_of = {}
        self.waited = {k: set() for k in self.E}
        self.rank = {}

    def _wait(self, eng, key, val, collect=None):
        if key[0] == "e":
            if key[1] == eng.name and eng.name == "pe":
                return
            sem = self.E[key[1]].sem
        else:
            sem = self.dma_sems[key[1]]
        if eng.known.get(key, 0) >= val:
            return
        if key[0] == "e":
            src = key[1]
            self.waited[src].add(val)
            res = (sem, lambda val=val, src=src: self.rank[src][val])
        else:
            res = (sem, lambda val=val: val)
        if collect is not None:
            collect.append(res)
        else:
            eng.prog.append(lambda h, res=res: h.wait_ge(res[0], res[1]()))
        eng.known[key] = val
        if key[0] == "e":
            h = self.E[key[1]].hist
            if val < len(h) and h[val]:
                for k2, v2 in h[val].items():
                    if eng.known.get(k2, 0) < v2:
                        eng.known[k2] = v2

    def _deps(self, eng, reads, writes, collect=None):
        best = {}

        def add(key, val):
            if best.get(key, 0) < val:
                best[key] = val

        for b in reads:
            if b.w is not None:
                add(b.w[0], b.w[1])
            for k, v in b.wset.items():
                add(k, v)
        for b in writes:
            if b.w is not None:
                add(b.w[0], b.w[1])
            for k, v in b.wset.items():
                add(k, v)
            for k, v in b.r.items():
                add(k, v)
        for k, v in sorted(best.items(), key=lambda kv: (kv[0][0] != "e", -kv[1])):
            self._wait(eng, k, v, collect)

    def _mark(self, key, val, reads, writes):
        for b in writes:
            b.w = (key, val)
            b.wset = {}
            b.r = {}
        for b in reads:
            if b.r.get(key, 0) < val:
                b.r[key] = val

    def op(self, engname, fn, reads=(), writes=()):
        ex = [b for b in reads if b.excl]
        if ex:
            reads = [b for b in reads if not b.excl]
            writes = list(writes) + [b for b in ex if b not in writes]
        eng = self.E[engname]
        fn = _freeze(fn)
        pend = []
        self._deps(eng, reads, writes, pend)
        for res in pend[:-1]:
            eng.prog.append(lambda h, res=res: h.wait_ge(res[0], res[1]()))
        last = pend[-1] if pend else None
        eng.count += 1
        eng.hist.append(dict(eng.known))

        def run(h, fn=fn, sem=eng.sem, idx=eng.count, w=self.waited[engname], last=last):
            ins = fn(h)
            if last is not None:
                ins._wait_ge(last[0], last[1]())
            if idx in w:
                ins.then_inc(sem, 1)
            return ins
        eng.prog.append(run)
        self._mark(("e", engname), eng.count, reads, writes)

    def dma(self, engname, slot, out, in_, reads=(), writes=(), parallel=False):
        eng = self.E[engname]
        if parallel:
            assert slot not in self.dma_sem_of
            idx = len(self.dma_sem_of)
            assert idx < len(self.dma_sems), "out of dma sems"
            self.dma_sem_of[slot] = [idx, 16]
            eng.prog.append(lambda h, out=out, in_=in_, sem=self.dma_sems[idx]:
                            h.dma_start(out=out, in_=in_).then_inc(sem, 16))
            for b in writes:
                b.wset[("d", idx)] = 16
            return (("d", idx), 16)
        self._deps(eng, reads, writes)
        if slot not in self.dma_sem_of:
            idx = len(self.dma_sem_of)
            assert idx < len(self.dma_sems), "out of dma sems"
            self.dma_sem_of[slot] = [idx, 0]
        ent = self.dma_sem_of[slot]
        ent[1] += 16
        eng.prog.append(lambda h, out=out, in_=in_, sem=self.dma_sems[ent[0]]:
                        h.dma_start(out=out, in_=in_).then_inc(sem, 16))
        self._mark(("d", ent[0]), ent[1], reads, writes)
        return (("d", ent[0]), ent[1])

    def barrier(self):
        for eng in self.E.values():
            for other in self.E.values():
                if other.count > 0 and other is not eng:
                    self._wait(eng, ("e", other.name), other.count)
            for slot, (idx, val) in self.dma_sem_of.items():
                self._wait(eng, ("d", idx), val)

    def emit(self):
        nc = self.nc
        E = self.E
        for name, w in self.waited.items():
            self.rank[name] = {v: i + 1 for i, v in enumerate(sorted(w))}
        with nc.Block() as block:
            @block.tensor
            def _(h):
                for f in E["pe"].prog:
                    f(h)

            @block.scalar
            def _(h):
                for f in E["act"].prog:
                    f(h)

            @block.vector
            def _(h):
                for f in E["dve"].prog:
                    f(h)

            @block.gpsimd
            def _(h):
                for f in E["pool"].prog:
                    f(h)

            @block.sync
            def _(h):
                for f in E["sp"].prog:
                    f(h)


def build_nc(n_heads=8, n_conv=8, dbg=False, stage=99):
    nc = bass.Bass("TRN2", target_bir_lowering=False)

    def din(name, shape):
        return nc.dram_tensor(name, list(shape), F32, kind="ExternalInput").ap()

    x_d = din("x", [L, D])
    wsg_d = din("wsg", [16, D, 512])
    wsm_d = din("wsm", [D, 16])
    wout_d = din("wout", [2048, D])
    cqw_d = din("cqw", [128, 96])
    cw_d = din("cw", [128, 24])
    cb_d = din("cb", [128, 8])
    nw_d = din("nw", [1, D])
    fnw_d = din("fnw", [1, D])
    gnw_d = din("gnw", [1, 128])
    alog_d = din("alog", [1, 8])
    dtb_d = din("dtb", [1, 8])
    ident_d = din("ident", [128, 128])
    ltri_d = din("ltri", [128, 128])
    mask_d = din("mask", [128, 256])
    sel_d = din("sel", [8, 4])
    y_d = nc.dram_tensor("y", [L, D], F32, kind="ExternalOutput").ap()
    if dbg:
        dbg_d = nc.dram_tensor("dbg", [128, 16, 2048], F32, kind="ExternalOutput").ap()

    with ExitStack() as es:
        def sb(name, shape, dt=F32):
            return es.enter_context(nc.sbuf_tensor("s_" + name, list(shape), dt))

        def ps(name, shape, dt=F32):
            return es.enter_context(nc.psum_tensor("p_" + name, list(shape), dt))

        sems = {k: es.enter_context(nc.semaphore(k)) for k in ["pe", "act", "dve", "pool", "sp"]}
        sems["dma"] = [es.enter_context(nc.semaphore(f"dma{i}")) for i in range(56)]
        fw = FW(nc, sems)

        hT = sb("hT", [128, 8, L], BF16)
        mixT = sb("mixT", [128, 16, L], BF16)
        ident = sb("ident", [128, 128])
        ident4 = sb("ident4", [128, 4, 128])
        selr = sb("selr", [8, 4], F32R)
        self32 = sb("self32", [8, 4])
        identb = sb("identb", [128, 128], BF16)
        ltri = sb("ltri", [128, 128])
        ones = sb("ones", [128, 128])
        maskb = sb("maskb", [128, 256], BF16)
        gnwb4 = sb("gnwb4", [128, 4, 128])
        cqw = sb("cqw", [128, 96])
        cw = sb("cw", [128, 24])
        cb = sb("cb", [128, 8])
        alog = sb("alog", [128, 8])
        dtb = sb("dtb", [128, 8])
        nexpA = sb("nexpA", [128, 8])
        wsm = sb("wsm", [128, 8, 16], BF16)
        b_hT = bufs("hT", NT)
        b_mix = [[Buf(f"mix{g}_{q}") for q in range(4)] for g in range(16)]
        b_const = Buf("const")
        b_small = Buf("small")

        PB = [ps(f"pb{i}", [128, 512]) for i in range(8)]

        fw.dma("sp", "c0", ident[:], ident_d, writes=[b_const], parallel=True)
        fw.dma("sp", "c1", ltri[:], ltri_d, writes=[b_const], parallel=True)
        for i in range(4):
            fw.dma("sp", f"c1b{i}", ident4[:, i, :], ident_d, writes=[b_const], parallel=True)
        fw.dma("sp", "c1c", self32[:], sel_d, writes=[b_const], parallel=True)
        fw.dma("pool", "c2", maskb[:], mask_d, writes=[b_const], parallel=True)
        for i in range(4):
            fw.dma("sp", f"c5{i}", gnwb4[:, i, :], gnw_d.partition_broadcast(128), writes=[b_const], parallel=True)
        fw.dma("sp", "c6", cqw[:], cqw_d, writes=[b_const], parallel=True)
        fw.dma("sp", "c7", cw[:], cw_d, writes=[b_const], parallel=True)
        fw.dma("sp", "c8", cb[:], cb_d, writes=[b_const], parallel=True)
        fw.dma("sp", "c9", alog[:], alog_d.partition_broadcast(128), writes=[b_const], parallel=True)
        fw.dma("sp", "c10", dtb[:], dtb_d.partition_broadcast(128), writes=[b_const], parallel=True)
        fw.dma("pool", "c11", wsm[:], wsm_d.rearrange("(c p) n -> p c n", p=128), writes=[b_const], parallel=True)
        fw.op("dve", lambda h: h.tensor_copy(out=identb[:], in_=ident[:]), reads=[b_const], writes=[b_small])
        fw.op("pool", lambda h: h.memset(ones[:], 1.0), writes=[b_small])
        fw.op("dve", lambda h: h.tensor_copy(out=selr[:], in_=self32[:]), reads=[b_const], writes=[b_small])
        fw.op("act", lambda h: h.activation(out=nexpA[:], in_=alog[:], func=AF.Exp), reads=[b_const], writes=[b_small])
        fw.op("dve", lambda h: h.tensor_scalar(out=nexpA[:], in0=nexpA[:], scalar1=-1.0, scalar2=None, op0=ALU.mult),
              reads=[b_small], writes=[b_small])

        with ExitStack() as es1:
            def sb1(name, shape, dt=F32):
                return es1.enter_context(nc.sbuf_tensor("s_" + name, list(shape), dt))
            nwb = sb1("nwb", [128, D])
            fw.dma("sp", "c3", nwb[:], nw_d.partition_broadcast(128), writes=[b_const], parallel=True)
            NXT = 6
            xt = [sb1(f"xt{i}", [128, D]) for i in range(NXT)]
            b_xt = bufs("xt", NXT)
            junk = sb1("junk1", [128, D], BF16)
            b_junk = Buf("junk")
            hb = [sb1(f"hb{i}", [128, D], BF16) for i in range(2)]
            b_hb = bufs("hb", 2)
            st1 = sb1("st1", [128, NT, 4])
            b_st1 = bufs("st1", NT)
            b_pt = [Buf("pt0", excl=True), Buf("pt1", excl=True)]

            def stage_dma(t):
                i3 = t % NXT
                fw.dma("sp", f"x{i3}", xt[i3][:], x_d[t * 128:(t + 1) * 128, :], writes=[b_xt[i3]])

            def stage_a(t):
                i3 = t % NXT
                fw.op("act", lambda h: h.activation(out=junk[:], in_=xt[i3][:], func=AF.Square, accum_out=st1[:, t, 0:1]),
                      reads=[b_xt[i3]], writes=[b_junk, b_st1[t]])
                fw.op("dve", lambda h: h.tensor_scalar(out=st1[:, t, 1:2], in0=st1[:, t, 0:1], scalar1=1.0 / D,
                                                       scalar2=EPS, op0=ALU.mult, op1=ALU.add),
                      reads=[b_st1[t]], writes=[b_st1[t]])
                fw.op("act", lambda h: h.activation(out=st1[:, t, 2:3], in_=st1[:, t, 1:2], func=AF.Ln),
                      reads=[b_st1[t]], writes=[b_st1[t]])
                fw.op("act", lambda h: h.activation(out=st1[:, t, 3:4], in_=st1[:, t, 2:3], func=AF.Exp, scale=-0.5),
                      reads=[b_st1[t]], writes=[b_st1[t]])

            def stage_b(t):
                i3 = t % NXT
                i = t % 2
                fw.op("dve", lambda h: h.scalar_tensor_tensor(out=hb[i][:], in0=xt[i3][:], scalar=st1[:, t, 3:4],
                                                              in1=nwb[:], op0=ALU.mult, op1=ALU.mult),
                      reads=[b_xt[i3], b_st1[t], b_const], writes=[b_hb[i]])
                pbv = PB[i][:, 0:512].bitcast(BF16)
                for c in range(8):
                    fw.op("pe", lambda h, c=c: h.transpose(out=pbv[:, c * 128:(c + 1) * 128], in_=hb[i][:, c * 128:(c + 1) * 128],
                                                           identity=identb[:]),
                          reads=[b_hb[i], b_small], writes=[b_pt[i]])

            def stage_c(t):
                i = t % 2
                pbv = PB[i][:, 0:512].bitcast(BF16)
                pv = pbv.rearrange("p (c n) -> p c n", c=8)
                fw.op("act", lambda h: h.copy(out=hT[:, 0:4, t * 128:(t + 1) * 128], in_=pv[:, 0:4, :]),
                      reads=[b_pt[i]], writes=[b_hT[t]])
                fw.op("dve", lambda h: h.tensor_copy(out=hT[:, 4:8, t * 128:(t + 1) * 128], in_=pv[:, 4:8, :]),
                      reads=[b_pt[i]], writes=[b_hT[t]])

            for t in range(4):
                stage_dma(t)
            stage_a(0)
            stage_a(1)
            for t in range(NT):
                if t + 4 < NT:
                    stage_dma(t + 4)
                if t + 2 < NT:
                    stage_a(t + 2)
                stage_b(t)
                if t >= 1:
                    stage_c(t - 1)
            stage_c(NT - 1)
        fw.barrier()

        es2 = ExitStack()

        def sb2(name, shape, dt=F32):
            return es2.enter_context(nc.sbuf_tensor("s_" + name, list(shape), dt))

        wbuf = [sb2(f"wbuf{i}", [128, 8, 512], BF16) for i in range(2)]
        b_wbuf = bufs("wbuf", 2)
        esG = ExitStack()

        def sbg(name, shape, dt=F32):
            return esG.enter_context(nc.sbuf_tensor("s_" + name, list(shape), dt))

        xp = [sbg(f"xp{i}", [128, 515], F32R) for i in range(3)]
        dg = sbg("dg", [128, 4, 128], F32R)
        b_dg = Buf("dg")
        b_xp = bufs("xp", 3)
        acc = [sbg(f"acc{i}", [128, 512]) for i in range(3)]
        b_acc = bufs("acc", 3)
        kqT = sbg("kqT", [128, 2, L], F32R)
        b_kT = bufs("kT", 4)
        b_qT = bufs("qT", 4)
        vT = sbg("vT", [128, L], F32R)
        b_vT = bufs("vT", 4)
        zs = sbg("zs", [128, NT, 128], BF16)
        b_zs = bufs("zs", 4)
        Ob4 = [sbg(f"Ob4{i}", [128, 4, 128]) for i in range(2)]
        b_Ob4 = bufs("Ob4", 2)
        bg = sbg("bg", [128, NT, 16])
        g3 = sbg("g3", [128, 8, NT])
        lnb = sbg("lnb", [128, 8, NT])
        tmpa = sbg("tmpa", [128, 8, NT])
        tmpb = sbg("tmpb", [128, 8, NT])
        gc = sbg("gc", [128, 8, NT])
        base1 = sbg("base1", [128, 8, NT])
        glb = sbg("glb", [128, 8, NT])
        eglb = sbg("eglb", [128, 8, NT])
        b_stat = Buf("stat")
        hs = sbg("hs", [128, 11, NT])
        b_hs = bufs("hs", 4)
        hso = sbg("hso", [128, 2, NT])
        b_hso = bufs("hso", 4)
        Xr = [sbg(f"Xr{i}", [128, 3, 8], F32R) for i in range(3)]
        b_Xr = bufs("Xr", 3)
        rows = [sbg(f"rows{i}", [8, 3, 128], F32R) for i in range(3)]
        b_rows = bufs("rows", 3)
        ktok4 = sbg("ktok4", [128, 4, 128])
        VK4 = sbg("VK4", [128, 4, 2, 128], F32R)
        Ks4 = sbg("Ks4", [128, 4, 128], F32R)
        ATQ4 = sbg("ATQ4", [128, 4, 256], F32R)
        PPall = sbg("PPall", [128, 2, 2, 2, 128], F32R)
        TTm4 = sbg("TTm4", [128, 4, 128], F32R)
        UW4 = sbg("UW4", [128, 4, 2, 128], F32R)
        GT4 = sbg("GT4", [128, 4, 128], F32R)
        AWn4 = sbg("AWn4", [128, 4, 128], F32R)
        b_ktok4, b_Kbg4, b_Ks4, b_Vb4 = Buf("ktok4"), Buf("Kbg4"), Buf("Ks4"), Buf("Vb4")
        b_ATQ4 = bufs("ATQ4", 2)
        b_PP, b_TTp = bufs("PP", 2), bufs("TTp", 2)
        b_UW, b_GT4, b_AWn4 = bufs("UW", 2), bufs("GT4", 4), Buf("AWn4")
        tmpO = [sbg(f"tmpO{i}", [128, 128]) for i in range(2)]
        b_tmpO = bufs("tmpO", 2)
        Sst = [sbg(f"S{i}", [128, 128], F32R) for i in range(2)]
        b_S = bufs("S", 2)
        ob4 = [sbg(f"ob4{i}", [128, 4, 128], BF16) for i in range(2)]
        b_ob4 = bufs("ob4", 2)
        junk2 = sbg("junk2", [128, 128])
        b_junk2 = Buf("junk2")

        bank = [Buf(f"bank{i}", excl=True) for i in range(8)]
        b_pacc = [bank[0], bank[1]]
        b_pz = bank[2]
        b_p3 = bank[3]

        conv_state = {"open": False}
        cv = {}
        p3 = {}
        b_fn = Buf("fnwb")
        b_wo = bufs("wo", 4)

        def open_conv():
            fw.barrier()
            if dbg:
                dq = sbg("dq", [128, 8, 128])
                b_dq = Buf("dq")
                fw.dma("sp", "dbg0", dbg_d[:, 0:2, :], kqT[:].bitcast(F32), reads=[])
                fw.dma("sp", "dbg1", dbg_d[:, 2, :], vT[:].bitcast(F32), reads=[])
                fw.dma("sp", "dbg2", dbg_d[:, 3, 0:512], Ob4[1][:].rearrange("p t n -> p (t n)"), reads=[])
                fw.op("dve", lambda h: h.tensor_copy(out=dq[:, 0, :], in_=g3[:].rearrange("p h t -> p (h t)")), writes=[b_dq])
                fw.op("dve", lambda h: h.tensor_copy(out=dq[:, 1, :], in_=gc[:].rearrange("p h t -> p (h t)")), writes=[b_dq])
                fw.op("dve", lambda h: h.tensor_copy(out=dq[:, 2, :], in_=lnb[:].rearrange("p h t -> p (h t)")), writes=[b_dq])
                fw.op("dve", lambda h: h.tensor_copy(out=dq[:, 3, 0:96], in_=hs[:, 0:6, :].rearrange("p h t -> p (h t)")), writes=[b_dq])
                fw.op("dve", lambda h: h.tensor_copy(out=dq[:, 4, 0:64], in_=hs[:, 6:10, :].rearrange("p h t -> p (h t)")), writes=[b_dq])
                fw.op("dve", lambda h: h.tensor_copy(out=dq[:, 5, :], in_=ATQ4[:, 3, 0:128].bitcast(F32)), writes=[b_dq])
                fw.op("dve", lambda h: h.tensor_copy(out=dq[:, 6, :], in_=TTm4[:, 3, :].bitcast(F32)), writes=[b_dq])
                fw.dma("sp", "dbg5", dbg_d[:, 6, 0:8 * 128], dq[:].rearrange("p t n -> p (t n)"), reads=[b_dq])
                fw.barrier()
            esG.close()
            conv_state["open"] = True
            esC = ExitStack()
            conv_state["es"] = esC

            def sbc(name, shape, dt=F32):
                return esC.enter_context(nc.sbuf_tensor("s_" + name, list(shape), dt))
            cv["Csb"] = [sbc(f"Csb{i}", [128, 512]) for i in range(2)]
            cv["xp3"] = sbc("xp3", [128, 514])
            cv["acc3"] = [sbc(f"acc3{i}", [128, 512]) for i in range(2)]
            cv["szc"] = [sbc(f"szc{i}", [128, 512]) for i in range(2)]
            cv["Bsb"] = [sbc(f"Bsb{i}", [128, 512]) for i in range(2)]
            p3["wo"] = sbc("wo", [128, 16, D], BF16)
            p3["fnwb"] = sbc("fnwb", [128, D])
            p3["xr"] = [sbc(f"xr{i}", [128, D]) for i in range(3)]
            p3["rr"] = [sbc(f"rr{i}", [128, D]) for i in range(2)]
            p3["junk3"] = sbc("junk3", [128, D], BF16)
            p3["yt"] = [sbc(f"yt{i}", [128, D]) for i in range(3)]
            p3["st3"] = sbc("st3", [128, NT, 4])
            fw.dma("sp", "c4", p3["fnwb"][:], fnw_d.partition_broadcast(128), writes=[b_fn])
            wov = wout_d.rearrange("(g p) n -> p g n", p=128)
            for q4 in range(4):
                fw.dma("pool", f"wo{q4}", p3["wo"][:, q4 * 4:(q4 + 1) * 4, :], wov[:, q4 * 4:(q4 + 1) * 4, :], writes=[b_wo[q4]])

        b_Csb = bufs("Csb", 2)
        b_xp3 = Buf("xp3")
        b_acc3 = bufs("acc3", 2)
        b_szc = bufs("szc", 2)
        b_Bsb = bufs("Bsb", 2)

        def load_w(sg, i):
            fw.dma("pool", f"w{i}", wbuf[i][:], wsg_d[sg].rearrange("(c p) n -> p c n", p=128), writes=[b_wbuf[i]])

        n_sg = 16
        active = [h for h in range(n_heads)] + [8 + c for c in range(n_conv)]
        if active:
            load_w(active[0], 0)

        for t in range(NT):
            for c in range(8):
                fw.op("pe", lambda h, t=t, c=c: h.matmul(PB[2][:, t * 16:(t + 1) * 16], lhsT=hT[:, c, t * 128:(t + 1) * 128],
                                                         rhs=wsm[:, c, :], start=(c == 0), stop=(c == 7)),
                      reads=[b_hT[t], b_const], writes=[b_pz])
        fw.op("act", lambda h: h.copy(out=bg[:].rearrange("p t n -> p (t n)"), in_=PB[2][:, 0:256]), reads=[b_pz], writes=[b_stat])
        bg_b = bg[:, :, 0:8].rearrange("p t h -> p h t")
        bg_a = bg[:, :, 8:16].rearrange("p t h -> p h t")
        fw.op("act", lambda h: h.activation(out=tmpa[:], in_=bg_b, func=AF.Exp, scale=-1.0), reads=[b_stat], writes=[b_stat])
        fw.op("act", lambda h: h.activation(out=lnb[:], in_=tmpa[:], func=AF.Ln, bias=1.0), reads=[b_stat], writes=[b_stat])
        fw.op("dve", lambda h: h.tensor_scalar(out=lnb[:], in0=lnb[:], scalar1=-1.0, scalar2=None, op0=ALU.mult),
              reads=[b_stat], writes=[b_stat])
        for hd in range(8):
            fw.op("dve", lambda h, hd=hd: h.tensor_scalar(out=tmpb[:, hd, :], in0=bg_a[:, hd, :], scalar1=dtb[:, hd:hd + 1],
                                                          scalar2=None, op0=ALU.add),
                  reads=[b_stat, b_const], writes=[b_stat])
        fw.op("act", lambda h: h.activation(out=tmpb[:], in_=tmpb[:], func=AF.Exp), reads=[b_stat], writes=[b_stat])
        fw.op("act", lambda h: h.activation(out=tmpb[:], in_=tmpb[:], func=AF.Ln, bias=1.0), reads=[b_stat], writes=[b_stat])
        for hd in range(8):
            fw.op("dve", lambda h, hd=hd: h.tensor_scalar(out=g3[:, hd, :], in0=tmpb[:, hd, :], scalar1=nexpA[:, hd:hd + 1],
                                                          scalar2=None, op0=ALU.mult),
                  reads=[b_stat, b_small], writes=[b_stat])
        g3f = g3[:].rearrange("p h t -> p (h t)")
        fw.op("pe", lambda h: h.matmul(PB[3][:, 0:128], lhsT=ltri[:], rhs=g3f, start=True, stop=True),
              reads=[b_stat, b_const], writes=[b_p3])
        fw.op("pe", lambda h: h.matmul(PB[3][:, 128:256], lhsT=ones[:], rhs=g3f, start=True, stop=True),
              reads=[b_stat, b_small], writes=[b_p3])
        fw.op("act", lambda h: h.copy(out=gc[:].rearrange("p h t -> p (h t)"), in_=PB[3][:, 0:128]), reads=[b_p3], writes=[b_stat])
        fw.op("act", lambda h: h.copy(out=glb[:].rearrange("p h t -> p (h t)"), in_=PB[3][:, 128:256]), reads=[b_p3], writes=[b_stat])
        fw.op("act", lambda h: h.activation(out=eglb[:], in_=glb[:], func=AF.Exp), reads=[b_stat], writes=[b_stat])
        fw.op("dve", lambda h: h.tensor_tensor(out=base1[:], in0=gc[:], in1=lnb[:], op=ALU.add), reads=[b_stat], writes=[b_stat])

        cnt = {"pacc": 0, "acc": 0, "gb": 0}

        def next_bank():
            pi = cnt["pacc"] % 2
            cnt["pacc"] += 1
            return pi

        def proj_fm(i_w, col0, tb, consume):
            pi = next_bank()
            for c in range(8):
                fw.op("pe", lambda h, c=c, pi=pi: h.matmul(PB[pi][:, :], lhsT=wbuf[i_w][:, c, col0:col0 + 128],
                                                           rhs=hT[:, c, tb * 512:(tb + 1) * 512],
                                                           start=(c == 0), stop=(c == 7)),
                      reads=[b_wbuf[i_w]] + b_hT[tb * 4:tb * 4 + 4], writes=[b_pacc[pi]])
            consume(pi)

        LNQ = float(np.log(128.0 ** -0.5))

        PCH_S4, PCH_S6, PCH_OTHER = 5, 3, 8
        pch = [4]

        def P_unit(hd, iw, tb, kb):
            t0 = tb * 4
            bsl = slice(tb * 512, (tb + 1) * 512)
            x = kb % 3
            dsts = [(kqT[:, 1, bsl], b_qT), (kqT[:, 0, bsl], b_kT), (vT[:, bsl], b_vT)]
            npe = [0]

            def tick():
                npe[0] += 1
                if npe[0] >= pch[0]:
                    npe[0] = 0
                    return True
                return False
            for s in range(3):
                if tb == 0:
                    fw.op("pool", lambda h, s=s: h.memset(xp[s][:, 0:3].bitcast(F32), 0.0), writes=[b_xp[s]])
                g = s * 8 + hd
                for j in range(4):
                    fw.op("pool", lambda h, j=j, g=g: h.tensor_scalar(out=dg[:, j, :], in0=ident[:], scalar1=cqw[:, j * 24 + g:j * 24 + g + 1],
                                                                    scalar2=1.0, op0=ALU.mult, op1=ALU.mult), reads=[b_const], writes=[b_dg])
                pa = next_bank()
                for c in range(8):
                    fw.op("pe", lambda h, c=c, pa=pa, s=s: h.matmul(PB[pa][:, :], lhsT=wbuf[iw][:, c, s * 128:(s + 1) * 128],
                                                                   rhs=hT[:, c, bsl], start=(c == 0), stop=(c == 7)),
                          reads=[b_wbuf[iw]] + b_hT[t0:t0 + 4], writes=[b_pacc[pa]])
                    if tick():
                        yield
                fw.op("act", lambda h, s=s, pa=pa: h.copy(out=xp[s][:, 3:515], in_=PB[pa][:, :]), reads=[b_pacc[pa]], writes=[b_xp[s]])
                yield
                pb = next_bank()
                for j in range(4):
                    fw.op("pe", lambda h, s=s, j=j, pb=pb: h.matmul(PB[pb][:, :], lhsT=dg[:, j, :], rhs=xp[s][:, j:j + 512],
                                                                   start=(j == 0), stop=(j == 3)),
                          reads=[b_dg, b_xp[s]], writes=[b_pacc[pb]])
                    if tick():
                        yield
                fw.op("dve", lambda h, s=s, pb=pb: h.tensor_copy(out=acc[s][:], in_=PB[pb][:, :]), reads=[b_pacc[pb]], writes=[b_acc[s]])
                fw.op("pool", lambda h, s=s: h.tensor_copy(out=xp[s][:, 0:3], in_=xp[s][:, 512:515].bitcast(F32)),
                      reads=[b_xp[s]], writes=[b_xp[s]])
            pz = next_bank()
            for tt in range(4):
                t = t0 + tt
                for c in range(8):
                    fw.op("pe", lambda h, t=t, tt=tt, c=c: h.matmul(PB[pz][:, tt * 128:(tt + 1) * 128],
                                                                   lhsT=hT[:, c, t * 128:(t + 1) * 128],
                                                                   rhs=wbuf[iw][:, c, 384:512], start=(c == 0), stop=(c == 7)),
                          reads=[b_hT[t], b_wbuf[iw]], writes=[b_pacc[pz]])
                    if tick():
                        yield
            for s in range(3):
                o_ap, bl = dsts[s]
                fw.op("act", lambda h, s=s, o_ap=o_ap: h.activation(out=o_ap, in_=acc[s][:], func=AF.Silu),
                      reads=[b_acc[s]], writes=[bl[tb]])
            zview = zs[:, t0:t0 + 4, :]
            fw.op("act", lambda h: h.activation(out=zview.rearrange("p a n -> p (a n)"), in_=PB[pz][:, :], func=AF.Silu),
                  reads=[b_pacc[pz]], writes=[b_zs[tb]])
            fw.op("pool", lambda h: h.tensor_tensor(out=zview, in0=zview, in1=gnwb4[:], op=ALU.mult),
                  reads=[b_zs[tb], b_const], writes=[b_zs[tb]])
            yield
            pq = next_bank()
            for s, bl in ((0, b_kT), (1, b_qT)):
                fw.op("act", lambda h, s=s: h.activation(out=acc[s][:], in_=kqT[:, s, bsl].bitcast(F32), func=AF.Square),
                      reads=[bl[tb]], writes=[b_acc[s]])
            for _ in range(3):
                yield
            for s, bl in ((0, b_kT), (1, b_qT)):
                for tt in range(4):
                    c0 = (s * 4 + tt) * 2
                    fw.op("pe", lambda h, s=s, tt=tt, c0=c0: h.matmul(PB[pq][:, c0:c0 + 2], lhsT=acc[s][:, tt * 128:(tt + 1) * 128],
                                                                     rhs=ones[:, 0:2], start=True, stop=True),
                          reads=[b_acc[s], b_small], writes=[b_pacc[pq]])
                yield
            bh = b_hs[tb]
            hsl = slice(t0, t0 + 4)
            fw.op("act", lambda h: h.activation(out=hs[:, 0:2, hsl],
                                                in_=PB[pq][:, 0:16].rearrange("p (s t two) -> p s t two", s=2, t=4, two=2)[:, :, :, 0],
                                                func=AF.Ln, bias=EPS), reads=[b_pacc[pq]], writes=[bh])
            fw.op("dve", lambda h: h.tensor_scalar(out=hs[:, 0, hsl], in0=hs[:, 0, hsl], scalar1=-0.5, scalar2=None, op0=ALU.mult),
                  reads=[bh], writes=[bh])
            fw.op("dve", lambda h: h.tensor_scalar(out=hs[:, 1, hsl], in0=hs[:, 1, hsl], scalar1=-0.5, scalar2=LNQ,
                                                   op0=ALU.mult, op1=ALU.add), reads=[bh], writes=[bh])
            fw.op("dve", lambda h: h.tensor_tensor(out=hs[:, 2, hsl], in0=base1[:, hd, hsl], in1=hs[:, 0, hsl], op=ALU.add),
                  reads=[bh, b_stat], writes=[bh])
            fw.op("dve", lambda h: h.tensor_tensor(out=hs[:, 3, hsl], in0=gc[:, hd, hsl], in1=hs[:, 1, hsl], op=ALU.add),
                  reads=[bh, b_stat], writes=[bh])
            fw.op("dve", lambda h: h.tensor_tensor(out=hs[:, 4, hsl], in0=hs[:, 0, hsl], in1=gc[:, hd, hsl], op=ALU.subtract),
                  reads=[bh, b_stat], writes=[bh])
            fw.op("dve", lambda h: h.tensor_tensor(out=hs[:, 5, hsl], in0=hs[:, 4, hsl], in1=glb[:, hd, hsl], op=ALU.add),
                  reads=[bh, b_stat], writes=[bh])
            fw.op("act", lambda h: h.activation(out=hs[:, 6, hsl], in_=hs[:, 2, hsl], func=AF.Exp), reads=[bh], writes=[bh])
            fw.op("dve", lambda h: h.tensor_scalar(out=hs[:, 10, hsl], in0=hs[:, 6, hsl], scalar1=-1.0, scalar2=None, op0=ALU.mult),
                  reads=[bh], writes=[bh])
            fw.op("act", lambda h: h.activation(out=hs[:, 7, hsl], in_=hs[:, 5, hsl], func=AF.Exp), reads=[bh], writes=[bh])
            fw.op("act", lambda h: h.activation(out=hs[:, 8, hsl], in_=lnb[:, hd, hsl], func=AF.Exp), reads=[bh, b_stat], writes=[bh])
            fw.op("act", lambda h: h.activation(out=hs[:, 9, hsl], in_=hs[:, 3, hsl], func=AF.Exp), reads=[bh], writes=[bh])
            fw.op("act", lambda h: h.copy(out=Xr[x][:, :, 0:4], in_=hs[:, 2:5, hsl]), reads=[bh], writes=[b_Xr[x]])
            fw.op("dve", lambda h: h.tensor_tensor(out=Xr[x][:, :, 4:8], in0=hs[:, 2:5, hsl], in1=Xr[x][:, :, 0:4].bitcast(F32),
                                                   op=ALU.subtract), reads=[bh, b_Xr[x]], writes=[b_Xr[x]])
            for _ in range(8):
                yield
            yield "TAIL"
            pr = next_bank()
            for k3 in range(3):
                fw.op("pe", lambda h, k3=k3: h.transpose(out=PB[pr][0:8, k3 * 128:(k3 + 1) * 128], in_=Xr[x][:, k3, :].bitcast(F32),
                                                         identity=ident[:]), reads=[b_Xr[x], b_const], writes=[b_pacc[pr]])
            fw.op("act", lambda h: h.copy(out=rows[x][:].rearrange("p a n -> p (a n)"), in_=PB[pr][0:8, 0:384]),
                  reads=[b_pacc[pr]], writes=[b_rows[x]])
            yield

        pending = []

        def G_unit(hd, tg, kb):
            t0 = tg * 4
            pch[0] = PCH_OTHER
            x = kb % 3
            oi = cnt["gb"] % 2
            cnt["gb"] += 1
            bh = b_hs[tg]
            if tg == 0:
                fw.op("pool", lambda h: h.memset(Sst[0][:].bitcast(F32), 0.0), writes=[b_S[0]])
            for j in range(4):
                tsl = slice((t0 + j) * 128, (t0 + j + 1) * 128)
                fw.op("pe", lambda h, j=j, tsl=tsl: h.transpose(out=PB[2][:, j * 128:(j + 1) * 128], in_=kqT[:, 0, tsl].bitcast(F32),
                                                                identity=ident[:]), reads=[b_kT[tg], b_const], writes=[bank[2]])
            for j in range(4):
                tsl = slice((t0 + j) * 128, (t0 + j + 1) * 128)
                fw.op("pe", lambda h, j=j, tsl=tsl: h.transpose(out=PB[3][:, j * 128:(j + 1) * 128], in_=vT[:, tsl].bitcast(F32),
                                                                identity=ident[:]), reads=[b_vT[tg], b_const], writes=[bank[3]])
            yield
            fw.op("act", lambda h: h.copy(out=ktok4[:].rearrange("p a n -> p (a n)"), in_=PB[2][:, :]), reads=[bank[2]], writes=[b_ktok4])

            def bc(row):
                return hs[:, row, t0:t0 + 4].unsqueeze(2).to_broadcast([128, 4, 128])
            fw.op("dve", lambda h: h.tensor_tensor(out=VK4[:, :, 0, :], in0=PB[3][:, :].rearrange("p (a n) -> p a n", a=4), in1=bc(8),
                                                   op=ALU.mult), reads=[bank[3], bh], writes=[b_Vb4])
            fw.op("pool", lambda h: h.tensor_tensor(out=VK4[:, :, 1, :], in0=ktok4[:], in1=bc(10), op=ALU.mult),
                  reads=[b_ktok4, bh], writes=[b_Kbg4])
            fw.op("pool", lambda h: h.tensor_tensor(out=Ks4[:], in0=ktok4[:], in1=bc(7), op=ALU.mult),
                  reads=[b_ktok4, bh], writes=[b_Ks4])
            for half in range(2):
                for jj in range(2):
                    j = half * 2 + jj
                    tsl = slice((t0 + j) * 128, (t0 + j + 1) * 128)
                    fw.op("pe", lambda h, half=half, jj=jj, tsl=tsl: h.matmul(
                        PB[4 + half][:, jj * 256:(jj + 1) * 256].rearrange("p (a n) -> p a n", a=2),
                        lhsT=kqT[:, 0, tsl], rhs=kqT[:, :, tsl], start=True, stop=True),
                        reads=[b_kT[tg], b_qT[tg]], writes=[bank[4 + half]])
                for jj in range(2):
                    j = half * 2 + jj
                    osl = slice(jj * 256, (jj + 1) * 256)
                    fw.op("pe", lambda h, half=half, osl=osl: h.matmul(PB[6 + half][:, osl], lhsT=identb[:], rhs=maskb[:],
                                                                      start=True, stop=False),
                          reads=[b_small, b_const], writes=[bank[6 + half]])
                    fw.op("pe", lambda h, half=half, osl=osl, j=j: h.matmul(
                        PB[6 + half][:, osl], lhsT=selr[:, j:j + 1].to_broadcast([8, 128]),
                        rhs=rows[x][:, 0:2, :].rearrange("p a n -> p (a n)"), start=False, stop=False),
                        reads=[b_small, b_rows[x]], writes=[bank[6 + half]])
                    fw.op("pe", lambda h, half=half, osl=osl, j=j: h.matmul(
                        PB[6 + half][:, osl], lhsT=rows[x][:, 2, :], rhs=selr[:, j:j + 1].to_broadcast([8, 256]),
                        start=False, stop=True),
                        reads=[b_small, b_rows[x]], writes=[bank[6 + half]])
                yield
            for half in range(2):
                asl = ATQ4[:, half * 2:half * 2 + 2, :].rearrange("p a n -> p (a n)")
                fw.op("act", lambda h, half=half, asl=asl: h.activation(out=asl, in_=PB[6 + half][:, :], func=AF.Exp),
                      reads=[bank[6 + half]], writes=[b_ATQ4[half]])
                fw.op("dve", lambda h, half=half, asl=asl: h.tensor_tensor(out=asl, in0=PB[4 + half][:, :], in1=asl.bitcast(F32),
                                                                           op=ALU.mult),
                      reads=[bank[4 + half], b_ATQ4[half]], writes=[b_ATQ4[half]])
            for j in range(4):
                fw.op("pe", lambda h, j=j: h.transpose(out=PB[2][:, j * 128:(j + 1) * 128], in_=ATQ4[:, j, 0:128].bitcast(F32),
                                                       identity=ident[:]), reads=[b_ATQ4[j // 2], b_const], writes=[bank[2]])
            yield
            fw.op("act", lambda h: h.copy(out=PPall[:, :, 0, :, :], in_=PB[2][:, :].rearrange("p (q t n) -> p q t n", q=2, t=2)),
                  reads=[bank[2]], writes=b_PP)
            fw.op("pool", lambda h: h.tensor_tensor(out=TTm4[:], in0=ident4[:], in1=ATQ4[:, :, 0:128].bitcast(F32), op=ALU.subtract),
                  reads=b_ATQ4 + [b_const], writes=b_TTp)
            def sq(q, m):
                for t in range(2):
                    j = 2 * q + t
                    ptp = ATQ4[:, j, 0:128] if m == 1 else PPall[:, q, 1, t, :]
                    rd = [b_PP[q]] + ([b_ATQ4[q]] if m == 1 else [])
                    fw.op("pe", lambda h, q=q, t=t, ptp=ptp: h.matmul(PB[3 + q][:, t * 128:(t + 1) * 128], lhsT=ptp,
                                                                     rhs=PPall[:, q, 0, t, :], start=True, stop=True),
                          reads=rd, writes=[bank[3 + q]])
                    if m < 6:
                        fw.op("pe", lambda h, q=q, t=t, ptp=ptp: h.matmul(PB[3 + q][:, 256 + t * 128:256 + (t + 1) * 128],
                                                                         lhsT=PPall[:, q, 0, t, :], rhs=ptp, start=True, stop=True),
                              reads=rd, writes=[bank[3 + q]])
                if m < 6:
                    fw.op("act", lambda h, q=q: h.copy(out=PPall[:, q, :, :, :].rearrange("p a t n -> p (a t n)"), in_=PB[3 + q][:, :]),
                          reads=[bank[3 + q]], writes=[b_PP[q]])
                else:
                    fw.op("act", lambda h, q=q: h.copy(out=PPall[:, q, 0, :, :].rearrange("p t n -> p (t n)"), in_=PB[3 + q][:, 0:256]),
                          reads=[bank[3 + q]], writes=[b_PP[q]])

            def prod(q):
                for t in range(2):
                    j = 2 * q + t
                    fw.op("pe", lambda h, q=q, t=t, j=j: h.matmul(PB[5 + q][:, t * 128:(t + 1) * 128], lhsT=PPall[:, q, 0, t, :],
                                                                 rhs=TTm4[:, j, :], start=True, stop=True),
                          reads=[b_PP[q], b_TTp[q]], writes=[bank[5 + q]])
                tsl2 = TTm4[:, 2 * q:2 * q + 2, :].rearrange("p a n -> p (a n)")
                fw.op("dve", lambda h, q=q, tsl2=tsl2: h.tensor_tensor(out=tsl2, in0=PB[5 + q][:, 0:256], in1=tsl2.bitcast(F32), op=ALU.add),
                      reads=[bank[5 + q], b_TTp[q]], writes=[b_TTp[q]])

            pch[0] = PCH_S4
            sq(0, 1)
            yield
            sq(1, 1)
            yield
            for m in range(2, 7):
                for q in range(2):
                    prod(q)
                    sq(q, m)
                    yield
                if m == 3:
                    while pending:
                        pending.pop(0)()
            prod(0)
            yield
            prod(1)
            yield
            pch[0] = PCH_OTHER
            for j in range(4):
                fw.op("pe", lambda h, j=j: h.matmul(PB[2 + j // 2][:, (j % 2) * 256:(j % 2 + 1) * 256].rearrange("p (a n) -> p a n", a=2),
                                                    lhsT=TTm4[:, j, :], rhs=VK4[:, j, :, :], start=True, stop=True),
                      reads=[b_TTp[j // 2], b_Vb4, b_Kbg4], writes=[bank[2 + j // 2]])
            yield 2
            fw.op("act", lambda h: h.copy(out=UW4[:, 0:2, :, :].rearrange("p t a n -> p (t a n)"), in_=PB[2][:, :]),
                  reads=[bank[2]], writes=[b_UW[0]])
            fw.op("dve", lambda h: h.tensor_copy(out=UW4[:, 2:4, :, :].rearrange("p t a n -> p (t a n)"), in_=PB[3][:, :]),
                  reads=[bank[3]], writes=[b_UW[1]])
            for j in range(4):
                fw.op("pe", lambda h, j=j: h.matmul(PB[4][:, j * 128:(j + 1) * 128], lhsT=Ks4[:, j, :], rhs=UW4[:, j, 0, :],
                                                    start=True, stop=True), reads=[b_Ks4, b_UW[j // 2]], writes=[bank[4]])
            for j in range(4):
                fw.op("pe", lambda h, j=j: h.matmul(PB[5][:, j * 128:(j + 1) * 128], lhsT=UW4[:, j, 1, :], rhs=Ks4[:, j, :],
                                                    start=True, stop=True), reads=[b_Ks4, b_UW[j // 2]], writes=[bank[5]])
            for j in range(4):
                fw.op("pe", lambda h, j=j: h.matmul(PB[6][:, j * 128:(j + 1) * 128], lhsT=UW4[:, j, 1, :], rhs=ATQ4[:, j, 128:256],
                                                    start=True, stop=True), reads=[b_ATQ4[j // 2], b_UW[j // 2]], writes=[bank[6]])
            yield 2
            for j in range(4):
                t = t0 + j
                fw.op("dve", lambda h, j=j, t=t: h.scalar_tensor_tensor(out=GT4[:, j, :], in0=ident[:], scalar=eglb[:, hd, t:t + 1],
                                                                       in1=PB[5][:, j * 128:(j + 1) * 128], op0=ALU.mult, op1=ALU.add),
                      reads=[bank[5], b_stat, b_const], writes=[b_GT4[j]])
            fw.op("act", lambda h: h.copy(out=ktok4[:].rearrange("p a n -> p (a n)"), in_=PB[4][:, :]), reads=[bank[4]], writes=[b_ktok4])
            fw.op("act", lambda h: h.copy(out=AWn4[:].rearrange("p a n -> p (a n)"), in_=PB[6][:, :]), reads=[bank[6]], writes=[b_AWn4])
            pch[0] = PCH_S6
            defer = []
            for j in range(4):
                t = t0 + j
                p = t % 2
                sc, sn = t % 2, (t + 1) % 2
                tsl = slice(t * 128, (t + 1) * 128)
                fw.op("pe", lambda h, j=j, sc=sc: h.matmul(PB[7][:, 0:128], lhsT=GT4[:, j, :], rhs=Sst[sc][:], start=True, stop=True),
                      reads=[b_GT4[j], b_S[sc]], writes=[bank[7]])
                qb = 2 if p == 0 else 4
                ob = 3 if p == 0 else 5
                fw.op("pe", lambda h, tsl=tsl, sc=sc, qb=qb: h.matmul(PB[qb][:, 0:128], lhsT=kqT[:, 1, tsl], rhs=Sst[sc][:],
                                                                      start=True, stop=True),
                      reads=[b_qT[tg], b_S[sc]], writes=[bank[qb]])
                fw.op("pe", lambda h, j=j, ob=ob: h.matmul(PB[ob][:, 0:128], lhsT=ATQ4[:, j, 128:256], rhs=UW4[:, j, 0, :],
                                                           start=True, stop=False),
                      reads=[b_ATQ4[j // 2], b_UW[j // 2]], writes=[bank[ob]])
                fw.op("pe", lambda h, j=j, ob=ob, sc=sc: h.matmul(PB[ob][:, 0:128], lhsT=AWn4[:, j, :], rhs=Sst[sc][:],
                                                                  start=False, stop=True),
                      reads=[b_AWn4, b_S[sc]], writes=[bank[ob]])
                yield 1
                fw.op("dve", lambda h, j=j, sn=sn: h.tensor_tensor(out=Sst[sn][:], in0=PB[7][:, 0:128], in1=ktok4[:, j, :], op=ALU.add),
                      reads=[bank[7], b_ktok4], writes=[b_S[sn]])
                for f in defer:
                    f()
                defer = []

                def off_chain(j=j, t=t, p=p, qb=qb, ob=ob):
                    fw.op("act", lambda h: h.mul(out=tmpO[p][:], in_=PB[qb][:, 0:128], mul=hs[:, 9, t:t + 1]),
                          reads=[bank[qb], bh], writes=[b_tmpO[p]])
                    fw.op("dve", lambda h: h.tensor_tensor(out=Ob4[oi][:, j, :], in0=PB[ob][:, 0:128], in1=tmpO[p][:],
                                                           op=ALU.add),
                          reads=[bank[ob], b_tmpO[p]], writes=[b_Ob4[oi]])
                    fw.op("act", lambda h: h.activation(out=junk2[:], in_=Ob4[oi][:, j, :], func=AF.Square,
                                                        accum_out=hso[:, 0, t:t + 1]),
                          reads=[b_Ob4[oi]], writes=[b_junk2, b_hso[tg]])
                defer.append(off_chain)
            for f in defer:
                f()
            pch[0] = PCH_OTHER

            fw.op("dve", lambda h: h.tensor_scalar(out=hso[:, 1, t0:t0 + 4], in0=hso[:, 0, t0:t0 + 4], scalar1=1.0 / 128,
                                                   scalar2=EPS, op0=ALU.mult, op1=ALU.add), reads=[b_hso[tg]], writes=[b_hso[tg]])
            fw.op("act", lambda h: h.activation(out=hso[:, 1, t0:t0 + 4], in_=hso[:, 1, t0:t0 + 4], func=AF.Ln),
                  reads=[b_hso[tg]], writes=[b_hso[tg]])
            fw.op("act", lambda h: h.activation(out=hso[:, 1, t0:t0 + 4], in_=hso[:, 1, t0:t0 + 4], func=AF.Exp, scale=-0.5),
                  reads=[b_hso[tg]], writes=[b_hso[tg]])
            for j in range(4):
                t = t0 + j
                fw.op("dve", lambda h, j=j, t=t: h.scalar_tensor_tensor(out=ob4[oi][:, j, :], in0=Ob4[oi][:, j, :],
                                                                       scalar=hso[:, 1, t:t + 1], in1=zs[:, t, :],
                                                                       op0=ALU.mult, op1=ALU.mult),
                      reads=[b_Ob4[oi], b_hso[tg], b_zs[tg]], writes=[b_ob4[oi]])

            def out_stage():
                pov = PB[7][:, 256:512].bitcast(BF16)
                for j in range(4):
                    fw.op("pe", lambda h, j=j: h.transpose(out=pov[:, j * 128:(j + 1) * 128], in_=ob4[oi][:, j, :], identity=identb[:]),
                          reads=[b_ob4[oi], b_small], writes=[bank[7]])
                fw.op("act", lambda h: h.copy(out=mixT[:, hd, t0 * 128:(t0 + 4) * 128], in_=pov), reads=[bank[7]], writes=[b_mix[hd][tg]])
            pending.append(out_stage)
            yield

        def run_all(gen):
            for _ in gen:
                pass

        stash = []

        def merge(G, P, ratio):
            k = 0
            g_alive, p_alive = True, P is not None
            while g_alive:
                try:
                    next(G)
                except StopIteration:
                    g_alive = False
                k += 1
                if k == 6:
                    while stash:
                        run_all(stash.pop(0))
                if p_alive:
                    try:
                        next(P)
                    except StopIteration:
                        p_alive = False
            while stash:
                run_all(stash.pop(0))
            if p_alive:
                for v in P:
                    if v == "TAIL":
                        stash.append(P)
                        break

        heads = [sg for sg in active if sg < 8]
        convs = [sg for sg in active if sg >= 8]
        blocks = []
        for idx, hd in enumerate(heads):
            for tb in range(4):
                blocks.append((idx, hd, tb))

        def start_P(k):
            idx, hd, tb = blocks[k]
            if tb == 0 and idx + 1 < len(active):
                load_w(active[idx + 1], (idx + 1) % 2)
            return P_unit(hd, idx % 2, tb, k)

        if blocks:
            run_all(start_P(0))
            if len(blocks) > 1:
                run_all(start_P(1))
            for k in range(len(blocks)):
                idx, hd, tb = blocks[k]
                Pn = start_P(k + 2) if k + 2 < len(blocks) else None
                merge(G_unit(hd, tb, k), Pn, 1)
            while stash:
                run_all(stash.pop(0))
            while pending:
                pending.pop(0)()

        for ci, sg in enumerate(convs):
            idx = len(heads) + ci
            iw = idx % 2
            if idx + 1 < len(active):
                load_w(active[idx + 1], (idx + 1) % 2)
            if True:
                c = sg - 8
                if not conv_state["open"]:
                    open_conv()
                    Csb, xp3, acc3, szc, Bsb = cv["Csb"], cv["xp3"], cv["acc3"], cv["szc"], cv["Bsb"]
                fw.op("pool", lambda h: h.memset(xp3[:, 0:2], 0.0), writes=[b_xp3])
                for tb in range(4):
                    st = {}

                    def cons_C(pi, st=st):
                        ci_ = cnt["acc"] % 2
                        st["ci"] = ci_
                        fw.op("act", lambda h: h.copy(out=Csb[ci_][:], in_=PB[pi][:, :]), reads=[b_pacc[pi]], writes=[b_Csb[ci_]])

                    def cons_h(pi, st=st, c=c):
                        ci_ = st["ci"]
                        ai = cnt["acc"] % 2
                        cnt["acc"] += 1
                        st["ai"] = ai
                        fw.op("dve", lambda h: h.tensor_tensor(out=xp3[:, 2:514], in0=PB[pi][:, :], in1=Csb[ci_][:], op=ALU.mult),
                              reads=[b_pacc[pi], b_Csb[ci_]], writes=[b_xp3])
                        fw.op("dve", lambda h: h.tensor_scalar(out=acc3[ai][:], in0=xp3[:, 2:514], scalar1=cw[:, 2 * 8 + c:2 * 8 + c + 1],
                                                               scalar2=cb[:, c:c + 1], op0=ALU.mult, op1=ALU.add),
                              reads=[b_xp3, b_const], writes=[b_acc3[ai]])
                        for j in (1, 0):
                            fw.op("dve", lambda h, j=j: h.scalar_tensor_tensor(out=acc3[ai][:], in0=xp3[:, j:j + 512],
                                                                               scalar=cw[:, j * 8 + c:j * 8 + c + 1],
                                                                               in1=acc3[ai][:], op0=ALU.mult, op1=ALU.add),
                                  reads=[b_xp3, b_const, b_acc3[ai]], writes=[b_acc3[ai]])
                        fw.op("dve", lambda h: h.tensor_copy(out=xp3[:, 0:2], in_=xp3[:, 512:514]), reads=[b_xp3], writes=[b_xp3])

                    def cons_B(pi, st=st):
                        ai = st["ai"]
                        fw.op("act", lambda h: h.copy(out=Bsb[ai][:], in_=PB[pi][:, :]), reads=[b_pacc[pi]], writes=[b_Bsb[ai]])
                        fw.op("pool", lambda h: h.tensor_tensor(out=acc3[ai][:], in0=Bsb[ai][:], in1=acc3[ai][:], op=ALU.mult),
                              reads=[b_Bsb[ai], b_acc3[ai]], writes=[b_acc3[ai]])

                    def cons_z(pi, st=st, c=c, tb=tb):
                        ai = st["ai"]
                        fw.op("act", lambda h: h.activation(out=szc[ai][:], in_=PB[pi][:, :], func=AF.Silu),
                              reads=[b_pacc[pi]], writes=[b_szc[ai]])
                        fw.op("pool", lambda h: h.tensor_tensor(out=mixT[:, 8 + c, tb * 512:(tb + 1) * 512], in0=acc3[ai][:],
                                                                in1=szc[ai][:], op=ALU.mult),
                              reads=[b_acc3[ai], b_szc[ai]], writes=[b_mix[8 + c][tb]])

                    proj_fm(iw, 128, tb, cons_C)
                    proj_fm(iw, 256, tb, cons_h)
                    proj_fm(iw, 0, tb, cons_B)
                    proj_fm(iw, 384, tb, cons_z)

        if not conv_state["open"]:
            open_conv()
        if dbg:
            fw.barrier()
            stg = conv_state["es"].enter_context(nc.sbuf_tensor("s_dbgstg", [128, 2048], F32))
            for slot, src in ((5, mixT[:, 0, :]), (7, mixT[:, 8, :])):
                fw.barrier()
                fw.op("dve", lambda h, src=src: h.tensor_copy(out=stg[:], in_=src))
                fw.barrier()
                fw.dma("sp", f"dbgs{slot}", dbg_d[:, slot, :], stg[:], reads=[])
            fw.barrier()
        if True:
            wo, fnwb, xr, rr, junk3, yt, st3 = p3["wo"], p3["fnwb"], p3["xr"], p3["rr"], p3["junk3"], p3["yt"], p3["st3"]
            b_xr = bufs("xr", 3)
            b_rr = bufs("rr", 2)
            b_junk3 = Buf("junk3")
            b_yt = bufs("yt", 3)
            b_st3 = bufs("st3", NT)
            last_tok = None
            for t in range(NT):
                i = t % 2
                i3 = t % 3
                tsl = slice(t * 128, (t + 1) * 128)
                if t == 0:
                    for tt in range(2):
                        fw.dma("sp", f"xr{tt % 3}", xr[tt % 3][:], x_d[tt * 128:(tt + 1) * 128, :], writes=[b_xr[tt % 3]])
                if t + 2 < NT:
                    tn = t + 2
                    fw.dma("sp", f"xr{tn % 3}", xr[tn % 3][:], x_d[tn * 128:(tn + 1) * 128, :], writes=[b_xr[tn % 3]])
                for half in range(2):
                    for g in range(16):
                        fw.op("pe", lambda h, g=g, half=half, tsl=tsl: h.matmul(PB[half][:, :], lhsT=mixT[:, g, tsl],
                                                                               rhs=wo[:, g, half * 512:(half + 1) * 512],
                                                                               start=(g == 0), stop=(g == 15)),
                              reads=[b_mix[g][t // 4], b_wo[g // 4]], writes=[b_pacc[half]])
                    fw.op("dve", lambda h, i=i, i3=i3, half=half: h.tensor_tensor(out=rr[i][:, half * 512:(half + 1) * 512],
                                                                           in0=PB[half][:, :],
                                                                           in1=xr[i3][:, half * 512:(half + 1) * 512], op=ALU.add),
                          reads=[b_pacc[half], b_xr[i3]], writes=[b_rr[i]])
                fw.op("act", lambda h, i=i, t=t: h.activation(out=junk3[:], in_=rr[i][:], func=AF.Square, accum_out=st3[:, t, 0:1]),
                      reads=[b_rr[i]], writes=[b_junk3, b_st3[t]])
                fw.op("dve", lambda h, t=t: h.tensor_scalar(out=st3[:, t, 1:2], in0=st3[:, t, 0:1], scalar1=1.0 / D, scalar2=EPS,
                                                            op0=ALU.mult, op1=ALU.add), reads=[b_st3[t]], writes=[b_st3[t]])
                fw.op("act", lambda h, t=t: h.activation(out=st3[:, t, 2:3], in_=st3[:, t, 1:2], func=AF.Ln),
                      reads=[b_st3[t]], writes=[b_st3[t]])
                fw.op("act", lambda h, t=t: h.activation(out=st3[:, t, 3:4], in_=st3[:, t, 2:3], func=AF.Exp, scale=-0.5),
                      reads=[b_st3[t]], writes=[b_st3[t]])
                fw.op("dve", lambda h, i=i, i3=i3, t=t: h.scalar_tensor_tensor(out=yt[i3][:], in0=rr[i][:], scalar=st3[:, t, 3:4],
                                                                         in1=fnwb[:], op0=ALU.mult, op1=ALU.mult),
                      reads=[b_rr[i], b_st3[t], b_fn], writes=[b_yt[i3]])
                last_tok = fw.dma("sp", f"y{i3}", y_d[tsl, :], yt[i3][:], reads=[b_yt[i3]])
            fw.barrier()
        conv_state["es"].close()
        es2.close()
        fw.emit()
    return nc


_NC_CACHE = {}


def _host_layout(inputs):
    f = np.float32
    w_in = np.asarray(inputs["w_in"][0], dtype=f)
    wsg = np.empty((16, D, 512), dtype=f)
    for hd in range(8):
        for s in range(4):
            wsg[hd, :, s * 128:(s + 1) * 128] = w_in[:, s * 1024 + hd * 128: s * 1024 + (hd + 1) * 128]
    for c in range(8):
        for s in range(4):
            base = 4112 + s * 1024
            wsg[8 + c, :, s * 128:(s + 1) * 128] = w_in[:, base + c * 128: base + (c + 1) * 128]
    wsm = np.ascontiguousarray(w_in[:, 4096:4112])
    cqw = np.ascontiguousarray(np.asarray(inputs["conv_qkv_w"][0], dtype=f).reshape(4, 24, 128).transpose(2, 0, 1)).reshape(128, 96)
    cw = np.ascontiguousarray(np.asarray(inputs["conv_w"][0], dtype=f).reshape(3, 8, 128).transpose(2, 0, 1)).reshape(128, 24)
    cb = np.ascontiguousarray(np.asarray(inputs["conv_b"][0], dtype=f).reshape(8, 128).T)
    i = np.arange(128)
    ident = np.eye(128, dtype=f)
    ltri = (i[:, None] <= i[None, :]).astype(f)
    maskS = np.where(i[None, :] > i[:, None], 0.0, NEG).astype(f)
    maskI = np.where(i[None, :] >= i[:, None], 0.0, NEG).astype(f)
    mask = np.concatenate([maskS, maskI], axis=1)
    sel = np.zeros((8, 4), dtype=f)
    sel[np.arange(4), np.arange(4)] = 1.0
    sel[4 + np.arange(4), np.arange(4)] = 1.0
    common = dict(
        wsg=wsg, wsm=wsm, wout=np.ascontiguousarray(inputs["w_out"][0], dtype=f),
        cqw=cqw, cw=cw, cb=cb,
        nw=np.asarray(inputs["norm_in_w"], dtype=f).reshape(1, D),
        fnw=np.asarray(inputs["final_norm_w"], dtype=f).reshape(1, D),
        gnw=np.asarray(inputs["gdn_norm_w"], dtype=f).reshape(1, 128),
        alog=np.asarray(inputs["A_log"], dtype=f).reshape(1, 8),
        dtb=np.asarray(inputs["dt_bias"], dtype=f).reshape(1, 8),
        ident=ident, ltri=ltri, mask=mask, sel=sel,
    )
    return common


def kernel(**inputs):
    x = np.asarray(inputs["x"], dtype=np.float32)
    common = _host_layout(inputs)
    if "nc" not in _NC_CACHE:
        _NC_CACHE["nc"] = build_nc()
    nc = _NC_CACHE["nc"]
    in_maps = [dict(common, x=np.ascontiguousarray(x[b])) for b in range(8)]
    res = run_bass_kernel_spmd(nc, in_maps, core_ids=list(range(8)))
    return np.stack([np.asarray(r["y"], dtype=np.float32) for r in res.results], axis=0)
```

```python
import types
import numpy as np
from contextlib import ExitStack
import concourse.bass as bass
import concourse.mybir as mybir
from concourse.bass_utils import run_bass_kernel_spmd

F32 = mybir.dt.float32
BF16 = mybir.dt.bfloat16
F32R = mybir.dt.float32r
AF = mybir.ActivationFunctionType
ALU = mybir.AluOpType

L = 2048
D = 1024
NT = 16
EPS = 1e-6
NEG = -30000.0


class Buf:
    __slots__ = ("name", "w", "r", "excl", "wset")

    def __init__(self, name, excl=False):
        self.name = name
        self.w = None
        self.wset = {}
        self.r = {}
        self.excl = excl


def _freeze(fn):
    if fn.__closure__ is None:
        return fn
    cells = []
    for c in fn.__closure__:
        try:
            cells.append(types.CellType(c.cell_contents))
        except ValueError:
            cells.append(c)
    return types.FunctionType(fn.__code__, fn.__globals__, fn.__name__, fn.__defaults__, tuple(cells))


def bufs(name, n):
    return [Buf(f"{name}{i}") for i in range(n)]


class Eng:
    def __init__(self, name, sem):
        self.name = name
        self.sem = sem
        self.count = 0
        self.known = {}
        self.prog = []
        self.hist = [None]


class FW:
    def __init__(self, nc, sems):
        self.nc = nc
        self.E = {k: Eng(k, sems[k]) for k in ("pe", "act", "dve", "pool", "sp")}
        self.dma_sems = list(sems["dma"])
        self.dma_sem_of = {}
        self.waited = {k: set() for k in self.E}
        self.rank = {}

    def _wait(self, eng, key, val, collect=None):
        if key[0] == "e":
            if key[1] == eng.name and eng.name == "pe":
                return
            sem = self.E[key[1]].sem
        else:
            sem = self.dma_sems[key[1]]
        if eng.known.get(key, 0) >= val:
            return
        if key[0] == "e":
            src = key[1]
            self.waited[src].add(val)
            res = (sem, lambda val=val, src=src: self.rank[src][val])
        else:
            res = (sem, lambda val=val: val)
        if collect is not None:
            collect.append(res)
        else:
            eng.prog.append(lambda h, res=res: h.wait_ge(res[0], res[1]()))
        eng.known[key] = val
        if key[0] == "e":
            h = self.E[key[1]].hist
            if val < len(h) and h[val]:
                for k2, v2 in h[val].items():
                    if eng.known.get(k2, 0) < v2:
                        eng.known[k2] = v2

    def _deps(self, eng, reads, writes, collect=None):
        best = {}

        def add(key, val):
            if best.get(key, 0) < val:
                best[key] = val

        for b in reads:
            if b.w is not None:
                add(b.w[0], b.w[1])
            for k, v in b.wset.items():
                add(k, v)
        for b in writes:
            if b.w is not None:
                add(b.w[0], b.w[1])
            for k, v in b.wset.items():
                add(k, v)
            for k, v in b.r.items():
                add(k, v)
        for k, v in sorted(best.items(), key=lambda kv: (kv[0][0] != "e", -kv[1])):
            self._wait(eng, k, v, collect)

    def _mark(self, key, val, reads, writes):
        for b in writes:
            b.w = (key, val)
            b.wset = {}
            b.r = {}
        for b in reads:
            if b.r.get(key, 0) < val:
                b.r[key] = val

    def op(self, engname, fn, reads=(), writes=()):
        ex = [b for b in reads if b.excl]
        if ex:
            reads = [b for b in reads if not b.excl]
            writes = list(writes) + [b for b in ex if b not in writes]
        eng = self.E[engname]
        fn = _freeze(fn)
        pend = []
        self._deps(eng, reads, writes, pend)
        for res in pend[:-1]:
            eng.prog.append(lambda h, res=res: h.wait_ge(res[0], res[1]()))
        last = pend[-1] if pend else None
        eng.count += 1
        eng.hist.append(dict(eng.known))

        def run(h, fn=fn, sem=eng.sem, idx=eng.count, w=self.waited[engname], last=last):
            ins = fn(h)
            if last is not None:
                ins._wait_ge(last[0], last[1]())
            if idx in w:
                ins.then_inc(sem, 1)
            return ins
        eng.prog.append(run)
        self._mark(("e", engname), eng.count, reads, writes)

    def dma(self, engname, slot, out, in_, reads=(), writes=(), parallel=False):
        eng = self.E[engname]
        if parallel:
            assert slot not in self.dma_sem_of
            idx = len(self.dma_sem_of)
            assert idx < len(self.dma_sems), "out of dma sems"
            self.dma_sem_of[slot] = [idx, 16]
            eng.prog.append(lambda h, out=out, in_=in_, sem=self.dma_sems[idx]:
                            h.dma_start(out=out, in_=in_).then_inc(sem, 16))
            for b in writes:
                b.wset[("d", idx)] = 16
            return (("d", idx), 16)
        self._deps(eng, reads, writes)
        if slot not in self.dma_sem_of:
            idx = len(self.dma_sem_of)
            assert idx < len(self.dma_sems), "out of dma sems"
            self.dma_sem_of[slot] = [idx, 0]
        ent = self.dma_sem_of[slot]
        ent[1] += 16
        eng.prog.append(lambda h, out=out, in_=in_, sem=self.dma_sems[ent[0]]:
                        h.dma_start(out=out, in_=in_).then_inc(sem, 16))
        self._mark(("d", ent[0]), ent[1], reads, writes)
        return (("d", ent[0]), ent[1])

    def barrier(self):
        for eng in self.E.values():
            for other in self.E.values():
                if other.count > 0 and other is not eng:
                    self._wait(eng, ("e", other.name), other.count)
            for slot, (idx, val) in self.dma_sem_of.items():
                self._wait(eng, ("d", idx), val)

    def emit(self):
        nc = self.nc
        E = self.E
        for name, w in self.waited.items():
            self.rank[name] = {v: i + 1 for i, v in enumerate(sorted(w))}
        with nc.Block() as block:
            @block.tensor
            def _(h):
                for f in E["pe"].prog:
                    f(h)

            @block.scalar
            def _(h):
                for f in E["act"].prog:
                    f(h)

            @block.vector
            def _(h):
                for f in E["dve"].prog:
                    f(h)

            @block.gpsimd
            def _(h):
                for f in E["pool"].prog:
                    f(h)

            @block.sync
            def _(h):
                for f in E["sp"].prog:
                    f(h)


def build_nc(n_heads=8, n_conv=8, dbg=False, stage=99):
    nc = bass.Bass("TRN2", target_bir_lowering=False)

    def din(name, shape):
        return nc.dram_tensor(name, list(shape), F32, kind="ExternalInput").ap()

    x_d = din("x", [L, D])
    wsg_d = din("wsg", [16, D, 512])
    wsm_d = din("wsm", [D, 16])
    wout_d = din("wout", [2048, D])
    cqw_d = din("cqw", [128, 96])
    cw_d = din("cw", [128, 24])
    cb_d = din("cb", [128, 8])
    nw_d = din("nw", [1, D])
    fnw_d = din("fnw", [1, D])
    gnw_d = din("gnw", [1, 128])
    alog_d = din("alog", [1, 8])
    dtb_d = din("dtb", [1, 8])
    ident_d = din("ident", [128, 128])
    ltri_d = din("ltri", [128, 128])
    mask_d = din("mask", [128, 256])
    sel_d = din("sel", [8, 4])
    y_d = nc.dram_tensor("y", [L, D], F32, kind="ExternalOutput").ap()
    if dbg:
        dbg_d = nc.dram_tensor("dbg", [128, 16, 2048], F32, kind="ExternalOutput").ap()

    with ExitStack() as es:
        def sb(name, shape, dt=F32):
            return es.enter_context(nc.sbuf_tensor("s_" + name, list(shape), dt))

        def ps(name, shape, dt=F32):
            return es.enter_context(nc.psum_tensor("p_" + name, list(shape), dt))

        sems = {k: es.enter_context(nc.semaphore(k)) for k in ["pe", "act", "dve", "pool", "sp"]}
        sems["dma"] = [es.enter_context(nc.semaphore(f"dma{i}")) for i in range(56)]
        fw = FW(nc, sems)

        hT = sb("hT", [128, 8, L], BF16)
        mixT = sb("mixT", [128, 16, L], BF16)
        ident = sb("ident", [128, 128])
        ident4 = sb("ident4", [128, 4, 128])
        selr = sb("selr", [8, 4], F32R)
        self32 = sb("self32", [8, 4])
        identb = sb("identb", [128, 128], BF16)
        ltri = sb("ltri", [128, 128])
        ones = sb("ones", [128, 128])
        maskb = sb("maskb", [128, 256], BF16)
        gnwb4 = sb("gnwb4", [128, 4, 128])
        cqw = sb("cqw", [128, 96])
        cw = sb("cw", [128, 24])
        cb = sb("cb", [128, 8])
        alog = sb("alog", [128, 8])
        dtb = sb("dtb", [128, 8])
        nexpA = sb("nexpA", [128, 8])
        wsm = sb("wsm", [128, 8, 16], BF16)
        b_hT = bufs("hT", NT)
        b_mix = [[Buf(f"mix{g}_{q}") for q in range(4)] for g in range(16)]
        b_const = Buf("const")
        b_small = Buf("small")

        PB = [ps(f"pb{i}", [128, 512]) for i in range(8)]

        fw.dma("sp", "c0", ident[:], ident_d, writes=[b_const], parallel=True)
        fw.dma("sp", "c1", ltri[:], ltri_d, writes=[b_const], parallel=True)
        for i in range(4):
            fw.dma("sp", f"c1b{i}", ident4[:, i, :], ident_d, writes=[b_const], parallel=True)
        fw.dma("sp", "c1c", self32[:], sel_d, writes=[b_const], parallel=True)
        fw.dma("pool", "c2", maskb[:], mask_d, writes=[b_const], parallel=True)
        for i in range(4):
            fw.dma("sp", f"c5{i}", gnwb4[:, i, :], gnw_d.partition_broadcast(128), writes=[b_const], parallel=True)
        fw.dma("sp", "c6", cqw[:], cqw_d, writes=[b_const], parallel=True)
        fw.dma("sp", "c7", cw[:], cw_d, writes=[b_const], parallel=True)
        fw.dma("sp", "c8", cb[:], cb_d, writes=[b_const], parallel=True)
        fw.dma("sp", "c9", alog[:], alog_d.partition_broadcast(128), writes=[b_const], parallel=True)
        fw.dma("sp", "c10", dtb[:], dtb_d.partition_broadcast(128), writes=[b_const], parallel=True)
        fw.dma("pool", "c11", wsm[:], wsm_d.rearrange("(c p) n -> p c n", p=128), writes=[b_const], parallel=True)
        fw.op("dve", lambda h: h.tensor_copy(out=identb[:], in_=ident[:]), reads=[b_const], writes=[b_small])
        fw.op("pool", lambda h: h.memset(ones[:], 1.0), writes=[b_small])
        fw.op("dve", lambda h: h.tensor_copy(out=selr[:], in_=self32[:]), reads=[b_const], writes=[b_small])
        fw.op("act", lambda h: h.activation(out=nexpA[:], in_=alog[:], func=AF.Exp), reads=[b_const], writes=[b_small])
        fw.op("dve", lambda h: h.tensor_scalar(out=nexpA[:], in0=nexpA[:], scalar1=-1.0, scalar2=None, op0=ALU.mult),
              reads=[b_small], writes=[b_small])

        with ExitStack() as es1:
            def sb1(name, shape, dt=F32):
                return es1.enter_context(nc.sbuf_tensor("s_" + name, list(shape), dt))
            nwb = sb1("nwb", [128, D])
            fw.dma("sp", "c3", nwb[:], nw_d.partition_broadcast(128), writes=[b_const], parallel=True)
            NXT = 6
            xt = [sb1(f"xt{i}", [128, D]) for i in range(NXT)]
            b_xt = bufs("xt", NXT)
            junk = sb1("junk1", [128, D], BF16)
            b_junk = Buf("junk")
            hb = [sb1(f"hb{i}", [128, D], BF16) for i in range(2)]
            b_hb = bufs("hb", 2)
            st1 = sb1("st1", [128, NT, 4])
            b_st1 = bufs("st1", NT)
            b_pt = [Buf("pt0", excl=True), Buf("pt1", excl=True)]

            def stage_dma(t):
                i3 = t % NXT
                fw.dma("sp", f"x{i3}", xt[i3][:], x_d[t * 128:(t + 1) * 128, :], writes=[b_xt[i3]])

            def stage_a(t):
                i3 = t % NXT
                fw.op("act", lambda h: h.activation(out=junk[:], in_=xt[i3][:], func=AF.Square, accum_out=st1[:, t, 0:1]),
                      reads=[b_xt[i3]], writes=[b_junk, b_st1[t]])
                fw.op("dve", lambda h: h.tensor_scalar(out=st1[:, t, 1:2], in0=st1[:, t, 0:1], scalar1=1.0 / D,
                                                       scalar2=EPS, op0=ALU.mult, op1=ALU.add),
                      reads=[b_st1[t]], writes=[b_st1[t]])
                fw.op("act", lambda h: h.activation(out=st1[:, t, 2:3], in_=st1[:, t, 1:2], func=AF.Ln),
                      reads=[b_st1[t]], writes=[b_st1[t]])
                fw.op("act", lambda h: h.activation(out=st1[:, t, 3:4], in_=st1[:, t, 2:3], func=AF.Exp, scale=-0.5),
                      reads=[b_st1[t]], writes=[b_st1[t]])

            def stage_b(t):
                i3 = t % NXT
                i = t % 2
                fw.op("dve", lambda h: h.scalar_tensor_tensor(out=hb[i][:], in0=xt[i3][:], scalar=st1[:, t, 3:4],
                                                              in1=nwb[:], op0=ALU.mult, op1=ALU.mult),
                      reads=[b_xt[i3], b_st1[t], b_const], writes=[b_hb[i]])
                pbv = PB[i][:, 0:512].bitcast(BF16)
                for c in range(8):
                    fw.op("pe", lambda h, c=c: h.transpose(out=pbv[:, c * 128:(c + 1) * 128], in_=hb[i][:, c * 128:(c + 1) * 128],
                                                           identity=identb[:]),
                          reads=[b_hb[i], b_small], writes=[b_pt[i]])

            def stage_c(t):
                i = t % 2
                pbv = PB[i][:, 0:512].bitcast(BF16)
                pv = pbv.rearrange("p (c n) -> p c n", c=8)
                fw.op("act", lambda h: h.copy(out=hT[:, 0:4, t * 128:(t + 1) * 128], in_=pv[:, 0:4, :]),
                      reads=[b_pt[i]], writes=[b_hT[t]])
                fw.op("dve", lambda h: h.tensor_copy(out=hT[:, 4:8, t * 128:(t + 1) * 128], in_=pv[:, 4:8, :]),
                      reads=[b_pt[i]], writes=[b_hT[t]])

            for t in range(4):
                stage_dma(t)
            stage_a(0)
            stage_a(1)
            for t in range(NT):
                if t + 4 < NT:
                    stage_dma(t + 4)
                if t + 2 < NT:
                    stage_a(t + 2)
                stage_b(t)
                if t >= 1:
                    stage_c(t - 1)
            stage_c(NT - 1)
        fw.barrier()

        es2 = ExitStack()

        def sb2(name, shape, dt=F32):
            return es2.enter_context(nc.sbuf_tensor("s_" + name, list(shape), dt))

        wbuf = [sb2(f"wbuf{i}", [128, 8, 512], BF16) for i in range(2)]
        b_wbuf = bufs("wbuf", 2)
        esG = ExitStack()

        def sbg(name, shape, dt=F32):
            return esG.enter_context(nc.sbuf_tensor("s_" + name, list(shape), dt))

        xp = [sbg(f"xp{i}", [128, 515], F32R) for i in range(3)]
        dg = sbg("dg", [128, 4, 128], F32R)
        b_dg = Buf("dg")
        b_xp = bufs("xp", 3)
        acc = [sbg(f"acc{i}", [128, 512]) for i in range(3)]
        b_acc = bufs("acc", 3)
        kqT = sbg("kqT", [128, 2, L], F32R)
        b_kT = bufs("kT", 4)
        b_qT = bufs("qT", 4)
        vT = sbg("vT", [128, L], F32R)
        b_vT = bufs("vT", 4)
        zs = sbg("zs", [128, NT, 128], BF16)
        b_zs = bufs("zs", 4)
        Ob4 = [sbg(f"Ob4{i}", [128, 4, 128]) for i in range(2)]
        b_Ob4 = bufs("Ob4", 2)
        bg = sbg("bg", [128, NT, 16])
        g3 = sbg("g3", [128, 8, NT])
        lnb = sbg("lnb", [128, 8, NT])
        tmpa = sbg("tmpa", [128, 8, NT])
        tmpb = sbg("tmpb", [128, 8, NT])
        gc = sbg("gc", [128, 8, NT])
        base1 = sbg("base1", [128, 8, NT])
        glb = sbg("glb", [128, 8, NT])
        eglb = sbg("eglb", [128, 8, NT])
        b_stat = Buf("stat")
        hs = sbg("hs", [128, 11, NT])
        b_hs = bufs("hs", 4)
        hso = sbg("hso", [128, 2, NT])
        b_hso = bufs("hso", 4)
        Xr = [sbg(f"Xr{i}", [128, 3, 8], F32R) for i in range(3)]
        b_Xr = bufs("Xr", 3)
        rows = [sbg(f"rows{i}", [8, 3, 128], F32R) for i in range(3)]
        b_rows = bufs("rows", 3)
        ktok4 = sbg("ktok4", [128, 4, 128])
        VK4 = sbg("VK4", [128, 4, 2, 128], F32R)
        Ks4 = sbg("Ks4", [128, 4, 128], F32R)
        ATQ4 = sbg("ATQ4", [128, 4, 256], F32R)
        PPall = sbg("PPall", [128, 2, 2, 2, 128], F32R)
        TTm4 = sbg("TTm4", [128, 4, 128], F32R)
        UW4 = sbg("UW4", [128, 4, 2, 128], F32R)
        GT4 = sbg("GT4", [128, 4, 128], F32R)
        AWn4 = sbg("AWn4", [128, 4, 128], F32R)
        b_ktok4, b_Kbg4, b_Ks4, b_Vb4 = Buf("ktok4"), Buf("Kbg4"), Buf("Ks4"), Buf("Vb4")
        b_ATQ4 = bufs("ATQ4", 2)
        b_PP, b_TTp = bufs("PP", 2), bufs("TTp", 2)
        b_UW, b_GT4, b_AWn4 = bufs("UW", 2), bufs("GT4", 4), Buf("AWn4")
        tmpO = [sbg(f"tmpO{i}", [128, 128]) for i in range(2)]
        b_tmpO = bufs("tmpO", 2)
        Sst = [sbg(f"S{i}", [128, 128], F32R) for i in range(2)]
        b_S = bufs("S", 2)
        ob4 = [sbg(f"ob4{i}", [128, 4, 128], BF16) for i in range(2)]
        b_ob4 = bufs("ob4", 2)
        junk2 = sbg("junk2", [128, 128])
        b_junk2 = Buf("junk2")

        bank = [Buf(f"bank{i}", excl=True) for i in range(8)]
        b_pacc = [bank[0], bank[1]]
        b_pz = bank[2]
        b_p3 = bank[3]

        conv_state = {"open": False}
        cv = {}
        p3 = {}
        b_fn = Buf("fnwb")
        b_wo = bufs("wo", 4)

        def open_conv():
            fw.barrier()
            if dbg:
                dq = sbg("dq", [128, 8, 128])
                b_dq = Buf("dq")
                fw.dma("sp", "dbg0", dbg_d[:, 0:2, :], kqT[:].bitcast(F32), reads=[])
                fw.dma("sp", "dbg1", dbg_d[:, 2, :], vT[:].bitcast(F32), reads=[])
                fw.dma("sp", "dbg2", dbg_d[:, 3, 0:512], Ob4[1][:].rearrange("p t n -> p (t n)"), reads=[])
                fw.op("dve", lambda h: h.tensor_copy(out=dq[:, 0, :], in_=g3[:].rearrange("p h t -> p (h t)")), writes=[b_dq])
                fw.op("dve", lambda h: h.tensor_copy(out=dq[:, 1, :], in_=gc[:].rearrange("p h t -> p (h t)")), writes=[b_dq])
                fw.op("dve", lambda h: h.tensor_copy(out=dq[:, 2, :], in_=lnb[:].rearrange("p h t -> p (h t)")), writes=[b_dq])
                fw.op("dve", lambda h: h.tensor_copy(out=dq[:, 3, 0:96], in_=hs[:, 0:6, :].rearrange("p h t -> p (h t)")), writes=[b_dq])
                fw.op("dve", lambda h: h.tensor_copy(out=dq[:, 4, 0:64], in_=hs[:, 6:10, :].rearrange("p h t -> p (h t)")), writes=[b_dq])
                fw.op("dve", lambda h: h.tensor_copy(out=dq[:, 5, :], in_=ATQ4[:, 3, 0:128].bitcast(F32)), writes=[b_dq])
                fw.op("dve", lambda h: h.tensor_copy(out=dq[:, 6, :], in_=TTm4[:, 3, :].bitcast(F32)), writes=[b_dq])
                fw.dma("sp", "dbg5", dbg_d[:, 6, 0:8 * 128], dq[:].rearrange("p t n -> p (t n)"), reads=[b_dq])
                fw.barrier()
            esG.close()
            conv_state["open"] = True
            esC = ExitStack()
            conv_state["es"] = esC

            def sbc(name, shape, dt=F32):
                return esC.enter_context(nc.sbuf_tensor("s_" + name, list(shape), dt))
            cv["Csb"] = [sbc(f"Csb{i}", [128, 512]) for i in range(2)]
            cv["xp3"] = sbc("xp3", [128, 514])
            cv["acc3"] = [sbc(f"acc3{i}", [128, 512]) for i in range(2)]
            cv["szc"] = [sbc(f"szc{i}", [128, 512]) for i in range(2)]
            cv["Bsb"] = [sbc(f"Bsb{i}", [128, 512]) for i in range(2)]
            p3["wo"] = sbc("wo", [128, 16, D], BF16)
            p3["fnwb"] = sbc("fnwb", [128, D])
            p3["xr"] = [sbc(f"xr{i}", [128, D]) for i in range(3)]
            p3["rr"] = [sbc(f"rr{i}", [128, D]) for i in range(2)]
            p3["junk3"] = sbc("junk3", [128, D], BF16)
            p3["yt"] = [sbc(f"yt{i}", [128, D]) for i in range(3)]
            p3["st3"] = sbc("st3", [128, NT, 4])
            fw.dma("sp", "c4", p3["fnwb"][:], fnw_d.partition_broadcast(128), writes=[b_fn])
            wov = wout_d.rearrange("(g p) n -> p g n", p=128)
            for q4 in range(4):
                fw.dma("pool", f"wo{q4}", p3["wo"][:, q4 * 4:(q4 + 1) * 4, :], wov[:, q4 * 4:(q4 + 1) * 4, :], writes=[b_wo[q4]])

        b_Csb = bufs("Csb", 2)
        b_xp3 = Buf("xp3")
        b_acc3 = bufs("acc3", 2)
        b_szc = bufs("szc", 2)
        b_Bsb = bufs("Bsb", 2)

        def load_w(sg, i):
            fw.dma("pool", f"w{i}", wbuf[i][:], wsg_d[sg].rearrange("(c p) n -> p c n", p=128), writes=[b_wbuf[i]])

        n_sg = 16
        active = [h for h in range(n_heads)] + [8 + c for c in range(n_conv)]
        if active:
            load_w(active[0], 0)

        for t in range(NT):
            for c in range(8):
                fw.op("pe", lambda h, t=t, c=c: h.matmul(PB[2][:, t * 16:(t + 1) * 16], lhsT=hT[:, c, t * 128:(t + 1) * 128],
                                                         rhs=wsm[:, c, :], start=(c == 0), stop=(c == 7)),
                      reads=[b_hT[t], b_const], writes=[b_pz])
        fw.op("act", lambda h: h.copy(out=bg[:].rearrange("p t n -> p (t n)"), in_=PB[2][:, 0:256]), reads=[b_pz], writes=[b_stat])
        bg_b = bg[:, :, 0:8].rearrange("p t h -> p h t")
        bg_a = bg[:, :, 8:16].rearrange("p t h -> p h t")
        fw.op("act", lambda h: h.activation(out=tmpa[:], in_=bg_b, func=AF.Exp, scale=-1.0), reads=[b_stat], writes=[b_stat])
        fw.op("act", lambda h: h.activation(out=lnb[:], in_=tmpa[:], func=AF.Ln, bias=1.0), reads=[b_stat], writes=[b_stat])
        fw.op("dve", lambda h: h.tensor_scalar(out=lnb[:], in0=lnb[:], scalar1=-1.0, scalar2=None, op0=ALU.mult),
              reads=[b_stat], writes=[b_stat])
        for hd in range(8):
            fw.op("dve", lambda h, hd=hd: h.tensor_scalar(out=tmpb[:, hd, :], in0=bg_a[:, hd, :], scalar1=dtb[:, hd:hd + 1],
                                                          scalar2=None, op0=ALU.add),
                  reads=[b_stat, b_const], writes=[b_stat])
        fw.op("act", lambda h: h.activation(out=tmpb[:], in_=tmpb[:], func=AF.Exp), reads=[b_stat], writes=[b_stat])
        fw.op("act", lambda h: h.activation(out=tmpb[:], in_=tmpb[:], func=AF.Ln, bias=1.0), reads=[b_stat], writes=[b_stat])
        for hd in range(8):
            fw.op("dve", lambda h, hd=hd: h.tensor_scalar(out=g3[:, hd, :], in0=tmpb[:, hd, :], scalar1=nexpA[:, hd:hd + 1],
                                                          scalar2=None, op0=ALU.mult),
                  reads=[b_stat, b_small], writes=[b_stat])
        g3f = g3[:].rearrange("p h t -> p (h t)")
        fw.op("pe", lambda h: h.matmul(PB[3][:, 0:128], lhsT=ltri[:], rhs=g3f, start=True, stop=True),
              reads=[b_stat, b_const], writes=[b_p3])
        fw.op("pe", lambda h: h.matmul(PB[3][:, 128:256], lhsT=ones[:], rhs=g3f, start=True, stop=True),
              reads=[b_stat, b_small], writes=[b_p3])
        fw.op("act", lambda h: h.copy(out=gc[:].rearrange("p h t -> p (h t)"), in_=PB[3][:, 0:128]), reads=[b_p3], writes=[b_stat])
        fw.op("act", lambda h: h.copy(out=glb[:].rearrange("p h t -> p (h t)"), in_=PB[3][:, 128:256]), reads=[b_p3], writes=[b_stat])
        fw.op("act", lambda h: h.activation(out=eglb[:], in_=glb[:], func=AF.Exp), reads=[b_stat], writes=[b_stat])
        fw.op("dve", lambda h: h.tensor_tensor(out=base1[:], in0=gc[:], in1=lnb[:], op=ALU.add), reads=[b_stat], writes=[b_stat])

        cnt = {"pacc": 0, "acc": 0, "gb": 0}

        def next_bank():
            pi = cnt["pacc"] % 2
            cnt["pacc"] += 1
            return pi

        def proj_fm(i_w, col0, tb, consume):
            pi = next_bank()
            for c in range(8):
                fw.op("pe", lambda h, c=c, pi=pi: h.matmul(PB[pi][:, :], lhsT=wbuf[i_w][:, c, col0:col0 + 128],
                                                           rhs=hT[:, c, tb * 512:(tb + 1) * 512],
                                                           start=(c == 0), stop=(c == 7)),
                      reads=[b_wbuf[i_w]] + b_hT[tb * 4:tb * 4 + 4], writes=[b_pacc[pi]])
            consume(pi)

        LNQ = float(np.log(128.0 ** -0.5))

        PCH_S4, PCH_S6, PCH_OTHER = 4, 4, 6
        pch = [4]

        def P_unit(hd, iw, tb, kb):
            t0 = tb * 4
            bsl = slice(tb * 512, (tb + 1) * 512)
            x = kb % 3
            dsts = [(kqT[:, 1, bsl], b_qT), (kqT[:, 0, bsl], b_kT), (vT[:, bsl], b_vT)]
            npe = [0]

            def tick():
                npe[0] += 1
                if npe[0] >= pch[0]:
                    npe[0] = 0
                    return True
                return False
            for s in range(3):
                if tb == 0:
                    fw.op("pool", lambda h, s=s: h.memset(xp[s][:, 0:3].bitcast(F32), 0.0), writes=[b_xp[s]])
                g = s * 8 + hd
                for j in range(4):
                    fw.op("pool", lambda h, j=j, g=g: h.tensor_scalar(out=dg[:, j, :], in0=ident[:], scalar1=cqw[:, j * 24 + g:j * 24 + g + 1],
                                                                    scalar2=1.0, op0=ALU.mult, op1=ALU.mult), reads=[b_const], writes=[b_dg])
                pa = next_bank()
                for c in range(8):
                    fw.op("pe", lambda h, c=c, pa=pa, s=s: h.matmul(PB[pa][:, :], lhsT=wbuf[iw][:, c, s * 128:(s + 1) * 128],
                                                                   rhs=hT[:, c, bsl], start=(c == 0), stop=(c == 7)),
                          reads=[b_wbuf[iw]] + b_hT[t0:t0 + 4], writes=[b_pacc[pa]])
                    if tick():
                        yield
                fw.op("act", lambda h, s=s, pa=pa: h.copy(out=xp[s][:, 3:515], in_=PB[pa][:, :]), reads=[b_pacc[pa]], writes=[b_xp[s]])
                yield
                pb = next_bank()
                for j in range(4):
                    fw.op("pe", lambda h, s=s, j=j, pb=pb: h.matmul(PB[pb][:, :], lhsT=dg[:, j, :], rhs=xp[s][:, j:j + 512],
                                                                   start=(j == 0), stop=(j == 3)),
                          reads=[b_dg, b_xp[s]], writes=[b_pacc[pb]])
                    if tick():
                        yield
                fw.op("dve", lambda h, s=s, pb=pb: h.tensor_copy(out=acc[s][:], in_=PB[pb][:, :]), reads=[b_pacc[pb]], writes=[b_acc[s]])
                fw.op("pool", lambda h, s=s: h.tensor_copy(out=xp[s][:, 0:3], in_=xp[s][:, 512:515].bitcast(F32)),
                      reads=[b_xp[s]], writes=[b_xp[s]])
            pz = next_bank()
            for tt in range(4):
                t = t0 + tt
                for c in range(8):
                    fw.op("pe", lambda h, t=t, tt=tt, c=c: h.matmul(PB[pz][:, tt * 128:(tt + 1) * 128],
                                                                   lhsT=hT[:, c, t * 128:(t + 1) * 128],
                                                                   rhs=wbuf[iw][:, c, 384:512], start=(c == 0), stop=(c == 7)),
                          reads=[b_hT[t], b_wbuf[iw]], writes=[b_pacc[pz]])
                    if tick():
                        yield
            for s in range(3):
                o_ap, bl = dsts[s]
                fw.op("act", lambda h, s=s, o_ap=o_ap: h.activation(out=o_ap, in_=acc[s][:], func=AF.Silu),
                      reads=[b_acc[s]], writes=[bl[tb]])
            zview = zs[:, t0:t0 + 4, :]
            fw.op("act", lambda h: h.activation(out=zview.rearrange("p a n -> p (a n)"), in_=PB[pz][:, :], func=AF.Silu),
                  reads=[b_pacc[pz]], writes=[b_zs[tb]])
            fw.op("pool", lambda h: h.tensor_tensor(out=zview, in0=zview, in1=gnwb4[:], op=ALU.mult),
                  reads=[b_zs[tb], b_const], writes=[b_zs[tb]])
            yield
            pq = next_bank()
            for s, bl in ((0, b_kT), (1, b_qT)):
                fw.op("act", lambda h, s=s: h.activation(out=acc[s][:], in_=kqT[:, s, bsl].bitcast(F32), func=AF.Square),
                      reads=[bl[tb]], writes=[b_acc[s]])
            for _ in range(3):
                yield
            for s, bl in ((0, b_kT), (1, b_qT)):
                for tt in range(4):
                    c0 = (s * 4 + tt) * 2
                    fw.op("pe", lambda h, s=s, tt=tt, c0=c0: h.matmul(PB[pq][:, c0:c0 + 2], lhsT=acc[s][:, tt * 128:(tt + 1) * 128],
                                                                     rhs=ones[:, 0:2], start=True, stop=True),
                          reads=[b_acc[s], b_small], writes=[b_pacc[pq]])
                yield
            bh = b_hs[tb]
            hsl = slice(t0, t0 + 4)
            fw.op("act", lambda h: h.activation(out=hs[:, 0:2, hsl],
                                                in_=PB[pq][:, 0:16].rearrange("p (s t two) -> p s t two", s=2, t=4, two=2)[:, :, :, 0],
                                                func=AF.Ln, bias=EPS), reads=[b_pacc[pq]], writes=[bh])
            fw.op("dve", lambda h: h.tensor_scalar(out=hs[:, 0, hsl], in0=hs[:, 0, hsl], scalar1=-0.5, scalar2=None, op0=ALU.mult),
                  reads=[bh], writes=[bh])
            fw.op("dve", lambda h: h.tensor_scalar(out=hs[:, 1, hsl], in0=hs[:, 1, hsl], scalar1=-0.5, scalar2=LNQ,
                                                   op0=ALU.mult, op1=ALU.add), reads=[bh], writes=[bh])
            fw.op("dve", lambda h: h.tensor_tensor(out=hs[:, 2, hsl], in0=base1[:, hd, hsl], in1=hs[:, 0, hsl], op=ALU.add),
                  reads=[bh, b_stat], writes=[bh])
            fw.op("dve", lambda h: h.tensor_tensor(out=hs[:, 3, hsl], in0=gc[:, hd, hsl], in1=hs[:, 1, hsl], op=ALU.add),
                  reads=[bh, b_stat], writes=[bh])
            fw.op("dve", lambda h: h.tensor_tensor(out=hs[:, 4, hsl], in0=hs[:, 0, hsl], in1=gc[:, hd, hsl], op=ALU.subtract),
                  reads=[bh, b_stat], writes=[bh])
            fw.op("dve", lambda h: h.tensor_tensor(out=hs[:, 5, hsl], in0=hs[:, 4, hsl], in1=glb[:, hd, hsl], op=ALU.add),
                  reads=[bh, b_stat], writes=[bh])
            fw.op("act", lambda h: h.activation(out=hs[:, 6, hsl], in_=hs[:, 2, hsl], func=AF.Exp), reads=[bh], writes=[bh])
            fw.op("dve", lambda h: h.tensor_scalar(out=hs[:, 10, hsl], in0=hs[:, 6, hsl], scalar1=-1.0, scalar2=None, op0=ALU.mult),
                  reads=[bh], writes=[bh])
            fw.op("act", lambda h: h.activation(out=hs[:, 7, hsl], in_=hs[:, 5, hsl], func=AF.Exp), reads=[bh], writes=[bh])
            fw.op("act", lambda h: h.activation(out=hs[:, 8, hsl], in_=lnb[:, hd, hsl], func=AF.Exp), reads=[bh, b_stat], writes=[bh])
            fw.op("act", lambda h: h.activation(out=hs[:, 9, hsl], in_=hs[:, 3, hsl], func=AF.Exp), reads=[bh], writes=[bh])
            fw.op("act", lambda h: h.copy(out=Xr[x][:, :, 0:4], in_=hs[:, 2:5, hsl]), reads=[bh], writes=[b_Xr[x]])
            fw.op("dve", lambda h: h.tensor_tensor(out=Xr[x][:, :, 4:8], in0=hs[:, 2:5, hsl], in1=Xr[x][:, :, 0:4].bitcast(F32),
                                                   op=ALU.subtract), reads=[bh, b_Xr[x]], writes=[b_Xr[x]])
            for _ in range(8):
                yield
            yield "TAIL"
            pr = next_bank()
            for k3 in range(3):
                fw.op("pe", lambda h, k3=k3: h.transpose(out=PB[pr][0:8, k3 * 128:(k3 + 1) * 128], in_=Xr[x][:, k3, :].bitcast(F32),
                                                         identity=ident[:]), reads=[b_Xr[x], b_const], writes=[b_pacc[pr]])
            fw.op("act", lambda h: h.copy(out=rows[x][:].rearrange("p a n -> p (a n)"), in_=PB[pr][0:8, 0:384]),
                  reads=[b_pacc[pr]], writes=[b_rows[x]])
            yield

        pending = []

        def G_unit(hd, tg, kb):
            t0 = tg * 4
            pch[0] = PCH_OTHER
            x = kb % 3
            oi = cnt["gb"] % 2
            cnt["gb"] += 1
            bh = b_hs[tg]
            if tg == 0:
                fw.op("pool", lambda h: h.memset(Sst[0][:].bitcast(F32), 0.0), writes=[b_S[0]])
            for j in range(4):
                tsl = slice((t0 + j) * 128, (t0 + j + 1) * 128)
                fw.op("pe", lambda h, j=j, tsl=tsl: h.transpose(out=PB[2][:, j * 128:(j + 1) * 128], in_=kqT[:, 0, tsl].bitcast(F32),
                                                                identity=ident[:]), reads=[b_kT[tg], b_const], writes=[bank[2]])
            for j in range(4):
                tsl = slice((t0 + j) * 128, (t0 + j + 1) * 128)
                fw.op("pe", lambda h, j=j, tsl=tsl: h.transpose(out=PB[3][:, j * 128:(j + 1) * 128], in_=vT[:, tsl].bitcast(F32),
                                                                identity=ident[:]), reads=[b_vT[tg], b_const], writes=[bank[3]])
            yield
            fw.op("act", lambda h: h.copy(out=ktok4[:].rearrange("p a n -> p (a n)"), in_=PB[2][:, :]), reads=[bank[2]], writes=[b_ktok4])

            def bc(row):
                return hs[:, row, t0:t0 + 4].unsqueeze(2).to_broadcast([128, 4, 128])
            fw.op("dve", lambda h: h.tensor_tensor(out=VK4[:, :, 0, :], in0=PB[3][:, :].rearrange("p (a n) -> p a n", a=4), in1=bc(8),
                                                   op=ALU.mult), reads=[bank[3], bh], writes=[b_Vb4])
            fw.op("pool", lambda h: h.tensor_tensor(out=VK4[:, :, 1, :], in0=ktok4[:], in1=bc(10), op=ALU.mult),
                  reads=[b_ktok4, bh], writes=[b_Kbg4])
            fw.op("pool", lambda h: h.tensor_tensor(out=Ks4[:], in0=ktok4[:], in1=bc(7), op=ALU.mult),
                  reads=[b_ktok4, bh], writes=[b_Ks4])
            for half in range(2):
                for jj in range(2):
                    j = half * 2 + jj
                    tsl = slice((t0 + j) * 128, (t0 + j + 1) * 128)
                    fw.op("pe", lambda h, half=half, jj=jj, tsl=tsl: h.matmul(
                        PB[4 + half][:, jj * 256:(jj + 1) * 256].rearrange("p (a n) -> p a n", a=2),
                        lhsT=kqT[:, 0, tsl], rhs=kqT[:, :, tsl], start=True, stop=True),
                        reads=[b_kT[tg], b_qT[tg]], writes=[bank[4 + half]])
                for jj in range(2):
                    j = half * 2 + jj
                    osl = slice(jj * 256, (jj + 1) * 256)
                    fw.op("pe", lambda h, half=half, osl=osl: h.matmul(PB[6 + half][:, osl], lhsT=identb[:], rhs=maskb[:],
                                                                      start=True, stop=False),
                          reads=[b_small, b_const], writes=[bank[6 + half]])
                    fw.op("pe", lambda h, half=half, osl=osl, j=j: h.matmul(
                        PB[6 + half][:, osl], lhsT=selr[:, j:j + 1].to_broadcast([8, 128]),
                        rhs=rows[x][:, 0:2, :].rearrange("p a n -> p (a n)"), start=False, stop=False),
                        reads=[b_small, b_rows[x]], writes=[bank[6 + half]])
                    fw.op("pe", lambda h, half=half, osl=osl, j=j: h.matmul(
                        PB[6 + half][:, osl], lhsT=rows[x][:, 2, :], rhs=selr[:, j:j + 1].to_broadcast([8, 256]),
                        start=False, stop=True),
                        reads=[b_small, b_rows[x]], writes=[bank[6 + half]])
                yield
            for half in range(2):
                asl = ATQ4[:, half * 2:half * 2 + 2, :].rearrange("p a n -> p (a n)")
                fw.op("act", lambda h, half=half, asl=asl: h.activation(out=asl, in_=PB[6 + half][:, :], func=AF.Exp),
                      reads=[bank[6 + half]], writes=[b_ATQ4[half]])
                fw.op("dve", lambda h, half=half, asl=asl: h.tensor_tensor(out=asl, in0=PB[4 + half][:, :], in1=asl.bitcast(F32),
                                                                           op=ALU.mult),
                      reads=[bank[4 + half], b_ATQ4[half]], writes=[b_ATQ4[half]])
            for j in range(4):
                fw.op("pe", lambda h, j=j: h.transpose(out=PB[2][:, j * 128:(j + 1) * 128], in_=ATQ4[:, j, 0:128].bitcast(F32),
                                                       identity=ident[:]), reads=[b_ATQ4[j // 2], b_const], writes=[bank[2]])
            yield
            fw.op("act", lambda h: h.copy(out=PPall[:, :, 0, :, :], in_=PB[2][:, :].rearrange("p (q t n) -> p q t n", q=2, t=2)),
                  reads=[bank[2]], writes=b_PP)
            fw.op("pool", lambda h: h.tensor_tensor(out=TTm4[:], in0=ident4[:], in1=ATQ4[:, :, 0:128].bitcast(F32), op=ALU.subtract),
                  reads=b_ATQ4 + [b_const], writes=b_TTp)
            def sq(q, m):
                for t in range(2):
                    j = 2 * q + t
                    ptp = ATQ4[:, j, 0:128] if m == 1 else PPall[:, q, 1, t, :]
                    rd = [b_PP[q]] + ([b_ATQ4[q]] if m == 1 else [])
                    fw.op("pe", lambda h, q=q, t=t, ptp=ptp: h.matmul(PB[3 + q][:, t * 128:(t + 1) * 128], lhsT=ptp,
                                                                     rhs=PPall[:, q, 0, t, :], start=True, stop=True),
                          reads=rd, writes=[bank[3 + q]])
                    if m < 6:
                        fw.op("pe", lambda h, q=q, t=t, ptp=ptp: h.matmul(PB[3 + q][:, 256 + t * 128:256 + (t + 1) * 128],
                                                                         lhsT=PPall[:, q, 0, t, :], rhs=ptp, start=True, stop=True),
                              reads=rd, writes=[bank[3 + q]])
                if m < 6:
                    fw.op("act", lambda h, q=q: h.copy(out=PPall[:, q, :, :, :].rearrange("p a t n -> p (a t n)"), in_=PB[3 + q][:, :]),
                          reads=[bank[3 + q]], writes=[b_PP[q]])
                else:
                    fw.op("act", lambda h, q=q: h.copy(out=PPall[:, q, 0, :, :].rearrange("p t n -> p (t n)"), in_=PB[3 + q][:, 0:256]),
                          reads=[bank[3 + q]], writes=[b_PP[q]])

            def prod(q):
                for t in range(2):
                    j = 2 * q + t
                    fw.op("pe", lambda h, q=q, t=t, j=j: h.matmul(PB[5 + q][:, t * 128:(t + 1) * 128], lhsT=PPall[:, q, 0, t, :],
                                                                 rhs=TTm4[:, j, :], start=True, stop=True),
                          reads=[b_PP[q], b_TTp[q]], writes=[bank[5 + q]])
                tsl2 = TTm4[:, 2 * q:2 * q + 2, :].rearrange("p a n -> p (a n)")
                fw.op("dve", lambda h, q=q, tsl2=tsl2: h.tensor_tensor(out=tsl2, in0=PB[5 + q][:, 0:256], in1=tsl2.bitcast(F32), op=ALU.add),
                      reads=[bank[5 + q], b_TTp[q]], writes=[b_TTp[q]])

            pch[0] = PCH_S4
            sq(0, 1)
            yield
            sq(1, 1)
            yield
            for m in range(2, 7):
                for q in range(2):
                    prod(q)
                    sq(q, m)
                    yield
                if m == 3:
                    while pending:
                        pending.pop(0)()
            prod(0)
            yield
            prod(1)
            yield
            pch[0] = PCH_OTHER
            for j in range(4):
                fw.op("pe", lambda h, j=j: h.matmul(PB[2 + j // 2][:, (j % 2) * 256:(j % 2 + 1) * 256].rearrange("p (a n) -> p a n", a=2),
                                                    lhsT=TTm4[:, j, :], rhs=VK4[:, j, :, :], start=True, stop=True),
                      reads=[b_TTp[j // 2], b_Vb4, b_Kbg4], writes=[bank[2 + j // 2]])
            yield 2
            fw.op("act", lambda h: h.copy(out=UW4[:, 0:2, :, :].rearrange("p t a n -> p (t a n)"), in_=PB[2][:, :]),
                  reads=[bank[2]], writes=[b_UW[0]])
            fw.op("dve", lambda h: h.tensor_copy(out=UW4[:, 2:4, :, :].rearrange("p t a n -> p (t a n)"), in_=PB[3][:, :]),
                  reads=[bank[3]], writes=[b_UW[1]])
            for j in range(4):
                fw.op("pe", lambda h, j=j: h.matmul(PB[4][:, j * 128:(j + 1) * 128], lhsT=Ks4[:, j, :], rhs=UW4[:, j, 0, :],
                                                    start=True, stop=True), reads=[b_Ks4, b_UW[j // 2]], writes=[bank[4]])
            for j in range(4):
                fw.op("pe", lambda h, j=j: h.matmul(PB[5][:, j * 128:(j + 1) * 128], lhsT=UW4[:, j, 1, :], rhs=Ks4[:, j, :],
                                                    start=True, stop=True), reads=[b_Ks4, b_UW[j // 2]], writes=[bank[5]])
            for j in range(4):
                fw.op("pe", lambda h, j=j: h.matmul(PB[6][:, j * 128:(j + 1) * 128], lhsT=UW4[:, j, 1, :], rhs=ATQ4[:, j, 128:256],
                                                    start=True, stop=True), reads=[b_ATQ4[j // 2], b_UW[j // 2]], writes=[bank[6]])
            yield 2
            for j in range(4):
                t = t0 + j
                fw.op("dve", lambda h, j=j, t=t: h.scalar_tensor_tensor(out=GT4[:, j, :], in0=ident[:], scalar=eglb[:, hd, t:t + 1],
                                                                       in1=PB[5][:, j * 128:(j + 1) * 128], op0=ALU.mult, op1=ALU.add),
                      reads=[bank[5], b_stat, b_const], writes=[b_GT4[j]])
            fw.op("act", lambda h: h.copy(out=ktok4[:].rearrange("p a n -> p (a n)"), in_=PB[4][:, :]), reads=[bank[4]], writes=[b_ktok4])
            fw.op("act", lambda h: h.copy(out=AWn4[:].rearrange("p a n -> p (a n)"), in_=PB[6][:, :]), reads=[bank[6]], writes=[b_AWn4])
            pch[0] = PCH_S6
            defer = []
            for j in range(4):
                t = t0 + j
                p = t % 2
                sc, sn = t % 2, (t + 1) % 2
                tsl = slice(t * 128, (t + 1) * 128)
                fw.op("pe", lambda h, j=j, sc=sc: h.matmul(PB[7][:, 0:128], lhsT=GT4[:, j, :], rhs=Sst[sc][:], start=True, stop=True),
                      reads=[b_GT4[j], b_S[sc]], writes=[bank[7]])
                qb = 2 if p == 0 else 4
                ob = 3 if p == 0 else 5
                fw.op("pe", lambda h, tsl=tsl, sc=sc, qb=qb: h.matmul(PB[qb][:, 0:128], lhsT=kqT[:, 1, tsl], rhs=Sst[sc][:],
                                                                      start=True, stop=True),
                      reads=[b_qT[tg], b_S[sc]], writes=[bank[qb]])
                fw.op("pe", lambda h, j=j, ob=ob: h.matmul(PB[ob][:, 0:128], lhsT=ATQ4[:, j, 128:256], rhs=UW4[:, j, 0, :],
                                                           start=True, stop=False),
                      reads=[b_ATQ4[j // 2], b_UW[j // 2]], writes=[bank[ob]])
                fw.op("pe", lambda h, j=j, ob=ob, sc=sc: h.matmul(PB[ob][:, 0:128], lhsT=AWn4[:, j, :], rhs=Sst[sc][:],
                                                                  start=False, stop=True),
                      reads=[b_AWn4, b_S[sc]], writes=[bank[ob]])
                yield 1
                fw.op("dve", lambda h, j=j, sn=sn: h.tensor_tensor(out=Sst[sn][:], in0=PB[7][:, 0:128], in1=ktok4[:, j, :], op=ALU.add),
                      reads=[bank[7], b_ktok4], writes=[b_S[sn]])
                for f in defer:
                    f()
                defer = []

                def off_chain(j=j, t=t, p=p, qb=qb, ob=ob):
                    fw.op("act", lambda h: h.mul(out=tmpO[p][:], in_=PB[qb][:, 0:128], mul=hs[:, 9, t:t + 1]),
                          reads=[bank[qb], bh], writes=[b_tmpO[p]])
                    fw.op("dve", lambda h: h.tensor_tensor(out=Ob4[oi][:, j, :], in0=PB[ob][:, 0:128], in1=tmpO[p][:],
                                                           op=ALU.add),
                          reads=[bank[ob], b_tmpO[p]], writes=[b_Ob4[oi]])
                    fw.op("act", lambda h: h.activation(out=junk2[:], in_=Ob4[oi][:, j, :], func=AF.Square,
                                                        accum_out=hso[:, 0, t:t + 1]),
                          reads=[b_Ob4[oi]], writes=[b_junk2, b_hso[tg]])
                defer.append(off_chain)
            for f in defer:
                f()
            pch[0] = PCH_OTHER

            fw.op("dve", lambda h: h.tensor_scalar(out=hso[:, 1, t0:t0 + 4], in0=hso[:, 0, t0:t0 + 4], scalar1=1.0 / 128,
                                                   scalar2=EPS, op0=ALU.mult, op1=ALU.add), reads=[b_hso[tg]], writes=[b_hso[tg]])
            fw.op("act", lambda h: h.activation(out=hso[:, 1, t0:t0 + 4], in_=hso[:, 1, t0:t0 + 4], func=AF.Ln),
                  reads=[b_hso[tg]], writes=[b_hso[tg]])
            fw.op("act", lambda h: h.activation(out=hso[:, 1, t0:t0 + 4], in_=hso[:, 1, t0:t0 + 4], func=AF.Exp, scale=-0.5),
                  reads=[b_hso[tg]], writes=[b_hso[tg]])
            for j in range(4):
                t = t0 + j
                fw.op("dve", lambda h, j=j, t=t: h.scalar_tensor_tensor(out=ob4[oi][:, j, :], in0=Ob4[oi][:, j, :],
                                                                       scalar=hso[:, 1, t:t + 1], in1=zs[:, t, :],
                                                                       op0=ALU.mult, op1=ALU.mult),
                      reads=[b_Ob4[oi], b_hso[tg], b_zs[tg]], writes=[b_ob4[oi]])

            def out_stage():
                pov = PB[7][:, 256:512].bitcast(BF16)
                for j in range(4):
                    fw.op("pe", lambda h, j=j: h.transpose(out=pov[:, j * 128:(j + 1) * 128], in_=ob4[oi][:, j, :], identity=identb[:]),
                          reads=[b_ob4[oi], b_small], writes=[bank[7]])
                fw.op("act", lambda h: h.copy(out=mixT[:, hd, t0 * 128:(t0 + 4) * 128], in_=pov), reads=[bank[7]], writes=[b_mix[hd][tg]])
            pending.append(out_stage)
            yield

        def run_all(gen):
            for _ in gen:
                pass

        stash = []

        def merge(G, P, ratio):
            k = 0
            g_alive, p_alive = True, P is not None
            while g_alive:
                try:
                    next(G)
                except StopIteration:
                    g_alive = False
                k += 1
                if k == 6:
                    while stash:
                        run_all(stash.pop(0))
                if p_alive:
                    try:
                        next(P)
                    except StopIteration:
                        p_alive = False
            while stash:
                run_all(stash.pop(0))
            if p_alive:
                for v in P:
                    if v == "TAIL":
                        stash.append(P)
                        break

        heads = [sg for sg in active if sg < 8]
        convs = [sg for sg in active if sg >= 8]
        blocks = []
        for idx, hd in enumerate(heads):
            for tb in range(4):
                blocks.append((idx, hd, tb))

        def start_P(k):
            idx, hd, tb = blocks[k]
            if tb == 0 and idx + 1 < len(active):
                load_w(active[idx + 1], (idx + 1) % 2)
            return P_unit(hd, idx % 2, tb, k)

        if blocks:
            run_all(start_P(0))
            if len(blocks) > 1:
                run_all(start_P(1))
            for k in range(len(blocks)):
                idx, hd, tb = blocks[k]
                Pn = start_P(k + 2) if k + 2 < len(blocks) else None
                merge(G_unit(hd, tb, k), Pn, 1)
            while stash:
                run_all(stash.pop(0))
            while pending:
                pending.pop(0)()

        for ci, sg in enumerate(convs):
            idx = len(heads) + ci
            iw = idx % 2
            if idx + 1 < len(active):
                load_w(active[idx + 1], (idx + 1) % 2)
            if True:
                c = sg - 8
                if not conv_state["open"]:
                    open_conv()
                    Csb, xp3, acc3, szc, Bsb = cv["Csb"], cv["xp3"], cv["acc3"], cv["szc"], cv["Bsb"]
                fw.op("pool", lambda h: h.memset(xp3[:, 0:2], 0.0), writes=[b_xp3])
                for tb in range(4):
                    st = {}

                    def cons_C(pi, st=st):
                        ci_ = cnt["acc"] % 2
                        st["ci"] = ci_
                        fw.op("act", lambda h: h.copy(out=Csb[ci_][:], in_=PB[pi][:, :]), reads=[b_pacc[pi]], writes=[b_Csb[ci_]])

                    def cons_h(pi, st=st, c=c):
                        ci_ = st["ci"]
                        ai = cnt["acc"] % 2
                        cnt["acc"] += 1
                        st["ai"] = ai
                        fw.op("dve", lambda h: h.tensor_tensor(out=xp3[:, 2:514], in0=PB[pi][:, :], in1=Csb[ci_][:], op=ALU.mult),
                              reads=[b_pacc[pi], b_Csb[ci_]], writes=[b_xp3])
                        fw.op("dve", lambda h: h.tensor_scalar(out=acc3[ai][:], in0=xp3[:, 2:514], scalar1=cw[:, 2 * 8 + c:2 * 8 + c + 1],
                                                               scalar2=cb[:, c:c + 1], op0=ALU.mult, op1=ALU.add),
                              reads=[b_xp3, b_const], writes=[b_acc3[ai]])
                        for j in (1, 0):
                            fw.op("dve", lambda h, j=j: h.scalar_tensor_tensor(out=acc3[ai][:], in0=xp3[:, j:j + 512],
                                                                               scalar=cw[:, j * 8 + c:j * 8 + c + 1],
                                                                               in1=acc3[ai][:], op0=ALU.mult, op1=ALU.add),
                                  reads=[b_xp3, b_const, b_acc3[ai]], writes=[b_acc3[ai]])
                        fw.op("dve", lambda h: h.tensor_copy(out=xp3[:, 0:2], in_=xp3[:, 512:514]), reads=[b_xp3], writes=[b_xp3])

                    def cons_B(pi, st=st):
                        ai = st["ai"]
                        fw.op("act", lambda h: h.copy(out=Bsb[ai][:], in_=PB[pi][:, :]), reads=[b_pacc[pi]], writes=[b_Bsb[ai]])
                        fw.op("pool", lambda h: h.tensor_tensor(out=acc3[ai][:], in0=Bsb[ai][:], in1=acc3[ai][:], op=ALU.mult),
                              reads=[b_Bsb[ai], b_acc3[ai]], writes=[b_acc3[ai]])

                    def cons_z(pi, st=st, c=c, tb=tb):
                        ai = st["ai"]
                        fw.op("act", lambda h: h.activation(out=szc[ai][:], in_=PB[pi][:, :], func=AF.Silu),
                              reads=[b_pacc[pi]], writes=[b_szc[ai]])
                        fw.op("pool", lambda h: h.tensor_tensor(out=mixT[:, 8 + c, tb * 512:(tb + 1) * 512], in0=acc3[ai][:],
                                                                in1=szc[ai][:], op=ALU.mult),
                              reads=[b_acc3[ai], b_szc[ai]], writes=[b_mix[8 + c][tb]])

                    proj_fm(iw, 128, tb, cons_C)
                    proj_fm(iw, 256, tb, cons_h)
                    proj_fm(iw, 0, tb, cons_B)
                    proj_fm(iw, 384, tb, cons_z)

        if not conv_state["open"]:
            open_conv()
        if dbg:
            fw.barrier()
            stg = conv_state["es"].enter_context(nc.sbuf_tensor("s_dbgstg", [128, 2048], F32))
            for slot, src in ((5, mixT[:, 0, :]), (7, mixT[:, 8, :])):
                fw.barrier()
                fw.op("dve", lambda h, src=src: h.tensor_copy(out=stg[:], in_=src))
                fw.barrier()
                fw.dma("sp", f"dbgs{slot}", dbg_d[:, slot, :], stg[:], reads=[])
            fw.barrier()
        if True:
            wo, fnwb, xr, rr, junk3, yt, st3 = p3["wo"], p3["fnwb"], p3["xr"], p3["rr"], p3["junk3"], p3["yt"], p3["st3"]
            b_xr = bufs("xr", 3)
            b_rr = bufs("rr", 2)
            b_junk3 = Buf("junk3")
            b_yt = bufs("yt", 3)
            b_st3 = bufs("st3", NT)
            last_tok = None
            for t in range(NT):
                i = t % 2
                i3 = t % 3
                tsl = slice(t * 128, (t + 1) * 128)
                if t == 0:
                    for tt in range(2):
                        fw.dma("sp", f"xr{tt % 3}", xr[tt % 3][:], x_d[tt * 128:(tt + 1) * 128, :], writes=[b_xr[tt % 3]])
                if t + 2 < NT:
                    tn = t + 2
                    fw.dma("sp", f"xr{tn % 3}", xr[tn % 3][:], x_d[tn * 128:(tn + 1) * 128, :], writes=[b_xr[tn % 3]])
                for half in range(2):
                    for g in range(16):
                        fw.op("pe", lambda h, g=g, half=half, tsl=tsl: h.matmul(PB[half][:, :], lhsT=mixT[:, g, tsl],
                                                                               rhs=wo[:, g, half * 512:(half + 1) * 512],
                                                                               start=(g == 0), stop=(g == 15)),
                              reads=[b_mix[g][t // 4], b_wo[g // 4]], writes=[b_pacc[half]])
                    fw.op("dve", lambda h, i=i, i3=i3, half=half: h.tensor_tensor(out=rr[i][:, half * 512:(half + 1) * 512],
                                                                           in0=PB[half][:, :],
                                                                           in1=xr[i3][:, half * 512:(half + 1) * 512], op=ALU.add),
                          reads=[b_pacc[half], b_xr[i3]], writes=[b_rr[i]])
                fw.op("act", lambda h, i=i, t=t: h.activation(out=junk3[:], in_=rr[i][:], func=AF.Square, accum_out=st3[:, t, 0:1]),
                      reads=[b_rr[i]], writes=[b_junk3, b_st3[t]])
                fw.op("dve", lambda h, t=t: h.tensor_scalar(out=st3[:, t, 1:2], in0=st3[:, t, 0:1], scalar1=1.0 / D, scalar2=EPS,
                                                            op0=ALU.mult, op1=ALU.add), reads=[b_st3[t]], writes=[b_st3[t]])
                fw.op("act", lambda h, t=t: h.activation(out=st3[:, t, 2:3], in_=st3[:, t, 1:2], func=AF.Ln),
                      reads=[b_st3[t]], writes=[b_st3[t]])
                fw.op("act", lambda h, t=t: h.activation(out=st3[:, t, 3:4], in_=st3[:, t, 2:3], func=AF.Exp, scale=-0.5),
                      reads=[b_st3[t]], writes=[b_st3[t]])
                fw.op("dve", lambda h, i=i, i3=i3, t=t: h.scalar_tensor_tensor(out=yt[i3][:], in0=rr[i][:], scalar=st3[:, t, 3:4],
                                                                         in1=fnwb[:], op0=ALU.mult, op1=ALU.mult),
                      reads=[b_rr[i], b_st3[t], b_fn], writes=[b_yt[i3]])
                last_tok = fw.dma("sp", f"y{i3}", y_d[tsl, :], yt[i3][:], reads=[b_yt[i3]])
            fw.barrier()
        conv_state["es"].close()
        es2.close()
        fw.emit()
    return nc


_NC_CACHE = {}


def _host_layout(inputs):
    f = np.float32
    w_in = np.asarray(inputs["w_in"][0], dtype=f)
    wsg = np.empty((16, D, 512), dtype=f)
    for hd in range(8):
        for s in range(4):
            wsg[hd, :, s * 128:(s + 1) * 128] = w_in[:, s * 1024 + hd * 128: s * 1024 + (hd + 1) * 128]
    for c in range(8):
        for s in range(4):
            base = 4112 + s * 1024
            wsg[8 + c, :, s * 128:(s + 1) * 128] = w_in[:, base + c * 128: base + (c + 1) * 128]
    wsm = np.ascontiguousarray(w_in[:, 4096:4112])
    cqw = np.ascontiguousarray(np.asarray(inputs["conv_qkv_w"][0], dtype=f).reshape(4, 24, 128).transpose(2, 0, 1)).reshape(128, 96)
    cw = np.ascontiguousarray(np.asarray(inputs["conv_w"][0], dtype=f).reshape(3, 8, 128).transpose(2, 0, 1)).reshape(128, 24)
    cb = np.ascontiguousarray(np.asarray(inputs["conv_b"][0], dtype=f).reshape(8, 128).T)
    i = np.arange(128)
    ident = np.eye(128, dtype=f)
    ltri = (i[:, None] <= i[None, :]).astype(f)
    maskS = np.where(i[None, :] > i[:, None], 0.0, NEG).astype(f)
    maskI = np.where(i[None, :] >= i[:, None], 0.0, NEG).astype(f)
    mask = np.concatenate([maskS, maskI], axis=1)
    sel = np.zeros((8, 4), dtype=f)
    sel[np.arange(4), np.arange(4)] = 1.0
    sel[4 + np.arange(4), np.arange(4)] = 1.0
    common = dict(
        wsg=wsg, wsm=wsm, wout=np.ascontiguousarray(inputs["w_out"][0], dtype=f),
        cqw=cqw, cw=cw, cb=cb,
        nw=np.asarray(inputs["norm_in_w"], dtype=f).reshape(1, D),
        fnw=np.asarray(inputs["final_norm_w"], dtype=f).reshape(1, D),
        gnw=np.asarray(inputs["gdn_norm_w"], dtype=f).reshape(1, 128),
        alog=np.asarray(inputs["A_log"], dtype=f).reshape(1, 8),
        dtb=np.asarray(inputs["dt_bias"], dtype=f).reshape(1, 8),
        ident=ident, ltri=ltri, mask=mask, sel=sel,
    )
    return common


def kernel(**inputs):
    x = np.asarray(inputs["x"], dtype=np.float32)
    common = _host_layout(inputs)
    if "nc" not in _NC_CACHE:
        _NC_CACHE["nc"] = build_nc()
    nc = _NC_CACHE["nc"]
    in_maps = [dict(common, x=np.ascontiguousarray(x[b])) for b in range(8)]
    res = run_bass_kernel_spmd(nc, in_maps, core_ids=list(range(8)))
    return np.stack([np.asarray(r["y"], dtype=np.float32) for r in res.results], axis=0)
```

```python
import types
import numpy as np
from contextlib import ExitStack
import concourse.bass as bass
import concourse.mybir as mybir
from concourse.bass_utils import run_bass_kernel_spmd

F32 = mybir.dt.float32
BF16 = mybir.dt.bfloat16
F32R = mybir.dt.float32r
AF = mybir.ActivationFunctionType
ALU = mybir.AluOpType

L = 2048
D = 1024
NT = 16
EPS = 1e-6
NEG = -30000.0


class Buf:
    __slots__ = ("name", "w", "r", "excl", "wset")

    def __init__(self, name, excl=False):
        self.name = name
        self.w = None
        self.wset = {}
        self.r = {}
        self.excl = excl


def _freeze(fn):
    if fn.__closure__ is None:
        return fn
    cells = []
    for c in fn.__closure__:
        try:
            cells.append(types.CellType(c.cell_contents))
        except ValueError:
            cells.append(c)
    return types.FunctionType(fn.__code__, fn.__globals__, fn.__name__, fn.__defaults__, tuple(cells))


def bufs(name, n):
    return [Buf(f"{name}{i}") for i in range(n)]


class Eng:
    def __init__(self, name, sem):
        self.name = name
        self.sem = sem
        self.count = 0
        self.known = {}
        self.prog = []
        self.hist = [None]


class FW:
    def __init__(self, nc, sems):
        self.nc = nc
        self.E = {k: Eng(k, sems[k]) for k in ("pe", "act", "dve", "pool", "sp")}
        self.dma_sems = list(sems["dma"])
        self.dma_sem_of = {}
        self.waited = {k: set() for k in self.E}
        self.rank = {}

    def _wait(self, eng, key, val, collect=None):
        if key[0] == "e":
            if key[1] == eng.name and eng.name == "pe":
                return
            sem = self.E[key[1]].sem
        else:
            sem = self.dma_sems[key[1]]
        if eng.known.get(key, 0) >= val:
            return
        if key[0] == "e":
            src = key[1]
            self.waited[src].add(val)
            res = (sem, lambda val=val, src=src: self.rank[src][val])
        else:
            res = (sem, lambda val=val: val)
        if collect is not None:
            collect.append(res)
        else:
            eng.prog.append(lambda h, res=res: h.wait_ge(res[0], res[1]()))
        eng.known[key] = val
        if key[0] == "e":
            h = self.E[key[1]].hist
            if val < len(h) and h[val]:
                for k2, v2 in h[val].items():
                    if eng.known.get(k2, 0) < v2:
                        eng.known[k2] = v2

    def _deps(self, eng, reads, writes, collect=None):
        best = {}

        def add(key, val):
            if best.get(key, 0) < val:
                best[key] = val

        for b in reads:
            if b.w is not None:
                add(b.w[0], b.w[1])
            for k, v in b.wset.items():
                add(k, v)
        for b in writes:
            if b.w is not None:
                add(b.w[0], b.w[1])
            for k, v in b.wset.items():
                add(k, v)
            for k, v in b.r.items():
                add(k, v)
        for k, v in sorted(best.items(), key=lambda kv: (kv[0][0] != "e", -kv[1])):
            self._wait(eng, k, v, collect)

    def _mark(self, key, val, reads, writes):
        for b in writes:
            b.w = (key, val)
            b.wset = {}
            b.r = {}
        for b in reads:
            if b.r.get(key, 0) < val:
                b.r[key] = val

    def op(self, engname, fn, reads=(), writes=()):
        ex = [b for b in reads if b.excl]
        if ex:
            reads = [b for b in reads if not b.excl]
            writes = list(writes) + [b for b in ex if b not in writes]
        eng = self.E[engname]
        fn = _freeze(fn)
        pend = []
        self._deps(eng, reads, writes, pend)
        for res in pend[:-1]:
            eng.prog.append(lambda h, res=res: h.wait_ge(res[0], res[1]()))
        last = pend[-1] if pend else None
        eng.count += 1
        eng.hist.append(dict(eng.known))

        def run(h, fn=fn, sem=eng.sem, idx=eng.count, w=self.waited[engname], last=last):
            ins = fn(h)
            if last is not None:
                ins._wait_ge(last[0], last[1]())
            if idx in w:
                ins.then_inc(sem, 1)
            return ins
        eng.prog.append(run)
        self._mark(("e", engname), eng.count, reads, writes)

    def dma(self, engname, slot, out, in_, reads=(), writes=(), parallel=False):
        eng = self.E[engname]
        if parallel:
            assert slot not in self.dma_sem_of
            idx = len(self.dma_sem_of)
            assert idx < len(self.dma_sems), "out of dma sems"
            self.dma_sem_of[slot] = [idx, 16]
            eng.prog.append(lambda h, out=out, in_=in_, sem=self.dma_sems[idx]:
                            h.dma_start(out=out, in_=in_).then_inc(sem, 16))
            for b in writes:
                b.wset[("d", idx)] = 16
            return (("d", idx), 16)
        self._deps(eng, reads, writes)
        if slot not in self.dma_sem_of:
            idx = len(self.dma_sem_of)
            assert idx < len(self.dma_sems), "out of dma sems"
            self.dma_sem_of[slot] = [idx, 0]
        ent = self.dma_sem_of[slot]
        ent[1] += 16
        eng.prog.append(lambda h, out=out, in_=in_, sem=self.dma_sems[ent[0]]:
                        h.dma_start(out=out, in_=in_).then_inc(sem, 16))
        self._mark(("d", ent[0]), ent[1], reads, writes)
        return (("d", ent[0]), ent[1])

    def barrier(self):
        for eng in self.E.values():
            for other in self.E.values():
                if other.count > 0 and other is not eng:
                    self._wait(eng, ("e", other.name), other.count)
            for slot, (idx, val) in self.dma_sem_of.items():
                self._wait(eng, ("d", idx), val)

    def emit(self):
        nc = self.nc
        E = self.E
        for name, w in self.waited.items():
            self.rank[name] = {v: i + 1 for i, v in enumerate(sorted(w))}
        with nc.Block() as block:
            @block.tensor
            def _(h):
                for f in E["pe"].prog:
                    f(h)

            @block.scalar
            def _(h):
                for f in E["act"].prog:
                    f(h)

            @block.vector
            def _(h):
                for f in E["dve"].prog:
                    f(h)

            @block.gpsimd
            def _(h):
                for f in E["pool"].prog:
                    f(h)

            @block.sync
            def _(h):
                for f in E["sp"].prog:
                    f(h)


def build_nc(n_heads=8, n_conv=8, dbg=False, stage=99):
    nc = bass.Bass("TRN2", target_bir_lowering=False)

    def din(name, shape):
        return nc.dram_tensor(name, list(shape), F32, kind="ExternalInput").ap()

    x_d = din("x", [L, D])
    wsg_d = din("wsg", [16, D, 512])
    wsm_d = din("wsm", [D, 16])
    wout_d = din("wout", [2048, D])
    cqw_d = din("cqw", [128, 96])
    cw_d = din("cw", [128, 24])
    cb_d = din("cb", [128, 8])
    nw_d = din("nw", [1, D])
    fnw_d = din("fnw", [1, D])
    gnw_d = din("gnw", [1, 128])
    alog_d = din("alog", [1, 8])
    dtb_d = din("dtb", [1, 8])
    ident_d = din("ident", [128, 128])
    ltri_d = din("ltri", [128, 128])
    mask_d = din("mask", [128, 256])
    sel_d = din("sel", [8, 4])
    y_d = nc.dram_tensor("y", [L, D], F32, kind="ExternalOutput").ap()
    if dbg:
        dbg_d = nc.dram_tensor("dbg", [128, 16, 2048], F32, kind="ExternalOutput").ap()

    with ExitStack() as es:
        def sb(name, shape, dt=F32):
            return es.enter_context(nc.sbuf_tensor("s_" + name, list(shape), dt))

        def ps(name, shape, dt=F32):
            return es.enter_context(nc.psum_tensor("p_" + name, list(shape), dt))

        sems = {k: es.enter_context(nc.semaphore(k)) for k in ["pe", "act", "dve", "pool", "sp"]}
        sems["dma"] = [es.enter_context(nc.semaphore(f"dma{i}")) for i in range(56)]
        fw = FW(nc, sems)

        hT = sb("hT", [128, 8, L], BF16)
        mixT = sb("mixT", [128, 16, L], BF16)
        ident = sb("ident", [128, 128])
        ident4 = sb("ident4", [128, 4, 128])
        selr = sb("selr", [8, 4], F32R)
        self32 = sb("self32", [8, 4])
        identb = sb("identb", [128, 128], BF16)
        ltri = sb("ltri", [128, 128])
        ones = sb("ones", [128, 128])
        onesr = sb("onesr", [128, 2], F32R)
        maskb = sb("maskb", [128, 256], BF16)
        gnwb4 = sb("gnwb4", [128, 4, 128])
        cqw = sb("cqw", [128, 96])
        cw = sb("cw", [128, 24])
        cb = sb("cb", [128, 8])
        alog = sb("alog", [128, 8])
        dtb = sb("dtb", [128, 8])
        nexpA = sb("nexpA", [128, 8])
        wsm = sb("wsm", [128, 8, 16], BF16)
        b_hT = bufs("hT", NT)
        b_mix = [[Buf(f"mix{g}_{q}") for q in range(4)] for g in range(16)]
        b_const = Buf("const")
        b_small = Buf("small")

        PB = [ps(f"pb{i}", [128, 512]) for i in range(8)]

        fw.dma("sp", "c0", ident[:], ident_d, writes=[b_const], parallel=True)
        fw.dma("sp", "c1", ltri[:], ltri_d, writes=[b_const], parallel=True)
        for i in range(4):
            fw.dma("sp", f"c1b{i}", ident4[:, i, :], ident_d, writes=[b_const], parallel=True)
        fw.dma("sp", "c1c", self32[:], sel_d, writes=[b_const], parallel=True)
        fw.dma("pool", "c2", maskb[:], mask_d, writes=[b_const], parallel=True)
        for i in range(4):
            fw.dma("sp", f"c5{i}", gnwb4[:, i, :], gnw_d.partition_broadcast(128), writes=[b_const], parallel=True)
        fw.dma("sp", "c6", cqw[:], cqw_d, writes=[b_const], parallel=True)
        fw.dma("sp", "c7", cw[:], cw_d, writes=[b_const], parallel=True)
        fw.dma("sp", "c8", cb[:], cb_d, writes=[b_const], parallel=True)
        fw.dma("sp", "c9", alog[:], alog_d.partition_broadcast(128), writes=[b_const], parallel=True)
        fw.dma("sp", "c10", dtb[:], dtb_d.partition_broadcast(128), writes=[b_const], parallel=True)
        fw.dma("pool", "c11", wsm[:], wsm_d.rearrange("(c p) n -> p c n", p=128), writes=[b_const], parallel=True)
        fw.op("dve", lambda h: h.tensor_copy(out=identb[:], in_=ident[:]), reads=[b_const], writes=[b_small])
        fw.op("pool", lambda h: h.memset(ones[:], 1.0), writes=[b_small])
        fw.op("dve", lambda h: h.tensor_copy(out=onesr[:], in_=ones[:, 0:2]), reads=[b_small], writes=[b_small])
        fw.op("dve", lambda h: h.tensor_copy(out=selr[:], in_=self32[:]), reads=[b_const], writes=[b_small])
        fw.op("act", lambda h: h.activation(out=nexpA[:], in_=alog[:], func=AF.Exp), reads=[b_const], writes=[b_small])
        fw.op("dve", lambda h: h.tensor_scalar(out=nexpA[:], in0=nexpA[:], scalar1=-1.0, scalar2=None, op0=ALU.mult),
              reads=[b_small], writes=[b_small])

        with ExitStack() as es1:
            def sb1(name, shape, dt=F32):
                return es1.enter_context(nc.sbuf_tensor("s_" + name, list(shape), dt))
            nwb = sb1("nwb", [128, D])
            fw.dma("sp", "c3", nwb[:], nw_d.partition_broadcast(128), writes=[b_const], parallel=True)
            NXT = 6
            xt = [sb1(f"xt{i}", [128, D]) for i in range(NXT)]
            b_xt = bufs("xt", NXT)
            junk = sb1("junk1", [128, D], BF16)
            b_junk = Buf("junk")
            hb = [sb1(f"hb{i}", [128, D], BF16) for i in range(2)]
            b_hb = bufs("hb", 2)
            st1 = sb1("st1", [128, NT, 4])
            b_st1 = bufs("st1", NT)
            b_pt = [Buf("pt0", excl=True), Buf("pt1", excl=True)]

            def stage_dma(t):
                i3 = t % NXT
                fw.dma("sp", f"x{i3}", xt[i3][:], x_d[t * 128:(t + 1) * 128, :], writes=[b_xt[i3]])

            def stage_a(t):
                i3 = t % NXT
                fw.op("act", lambda h: h.activation(out=junk[:], in_=xt[i3][:], func=AF.Square, accum_out=st1[:, t, 0:1]),
                      reads=[b_xt[i3]], writes=[b_junk, b_st1[t]])
                fw.op("dve", lambda h: h.tensor_scalar(out=st1[:, t, 1:2], in0=st1[:, t, 0:1], scalar1=1.0 / D,
                                                       scalar2=EPS, op0=ALU.mult, op1=ALU.add),
                      reads=[b_st1[t]], writes=[b_st1[t]])
                fw.op("act", lambda h: h.activation(out=st1[:, t, 2:3], in_=st1[:, t, 1:2], func=AF.Ln),
                      reads=[b_st1[t]], writes=[b_st1[t]])
                fw.op("act", lambda h: h.activation(out=st1[:, t, 3:4], in_=st1[:, t, 2:3], func=AF.Exp, scale=-0.5),
                      reads=[b_st1[t]], writes=[b_st1[t]])

            def stage_b(t):
                i3 = t % NXT
                i = t % 2
                fw.op("dve", lambda h: h.scalar_tensor_tensor(out=hb[i][:], in0=xt[i3][:], scalar=st1[:, t, 3:4],
                                                              in1=nwb[:], op0=ALU.mult, op1=ALU.mult),
                      reads=[b_xt[i3], b_st1[t], b_const], writes=[b_hb[i]])
                pbv = PB[i][:, 0:512].bitcast(BF16)
                for c in range(8):
                    fw.op("pe", lambda h, c=c: h.transpose(out=pbv[:, c * 128:(c + 1) * 128], in_=hb[i][:, c * 128:(c + 1) * 128],
                                                           identity=identb[:]),
                          reads=[b_hb[i], b_small], writes=[b_pt[i]])

            def stage_c(t):
                i = t % 2
                pbv = PB[i][:, 0:512].bitcast(BF16)
                pv = pbv.rearrange("p (c n) -> p c n", c=8)
                fw.op("act", lambda h: h.copy(out=hT[:, 0:4, t * 128:(t + 1) * 128], in_=pv[:, 0:4, :]),
                      reads=[b_pt[i]], writes=[b_hT[t]])
                fw.op("dve", lambda h: h.tensor_copy(out=hT[:, 4:8, t * 128:(t + 1) * 128], in_=pv[:, 4:8, :]),
                      reads=[b_pt[i]], writes=[b_hT[t]])

            for t in range(4):
                stage_dma(t)
            stage_a(0)
            stage_a(1)
            for t in range(NT):
                if t + 4 < NT:
                    stage_dma(t + 4)
                if t + 2 < NT:
                    stage_a(t + 2)
                stage_b(t)
                if t >= 1:
                    stage_c(t - 1)
            stage_c(NT - 1)
        fw.barrier()

        es2 = ExitStack()

        def sb2(name, shape, dt=F32):
            return es2.enter_context(nc.sbuf_tensor("s_" + name, list(shape), dt))

        wbuf = [sb2(f"wbuf{i}", [128, 8, 512], BF16) for i in range(2)]
        b_wbuf = bufs("wbuf", 2)
        esG = ExitStack()

        def sbg(name, shape, dt=F32):
            return esG.enter_context(nc.sbuf_tensor("s_" + name, list(shape), dt))

        xp = [sbg(f"xp{i}", [128, 515], F32R) for i in range(3)]
        dg = sbg("dg", [128, 4, 128], F32R)
        b_dg = Buf("dg")
        b_xp = bufs("xp", 3)
        acc = [sbg(f"acc{i}", [128, 512]) for i in range(3)]
        b_acc = bufs("acc", 3)
        kqT = sbg("kqT", [128, 2, L], F32R)
        b_kT = bufs("kT", 4)
        b_qT = bufs("qT", 4)
        vT = sbg("vT", [128, L], F32R)
        b_vT = bufs("vT", 4)
        zs = sbg("zs", [128, NT, 128], BF16)
        b_zs = bufs("zs", 4)
        Ob4 = [sbg(f"Ob4{i}", [128, 4, 128]) for i in range(2)]
        b_Ob4 = bufs("Ob4", 2)
        bg = sbg("bg", [128, NT, 16])
        g3 = sbg("g3", [128, 8, NT])
        lnb = sbg("lnb", [128, 8, NT])
        tmpa = sbg("tmpa", [128, 8, NT])
        tmpb = sbg("tmpb", [128, 8, NT])
        gc = sbg("gc", [128, 8, NT])
        base1 = sbg("base1", [128, 8, NT])
        glb = sbg("glb", [128, 8, NT])
        eglb = sbg("eglb", [128, 8, NT])
        b_stat = Buf("stat")
        hs = sbg("hs", [128, 11, NT])
        b_hs = bufs("hs", 4)
        hso = sbg("hso", [128, 2, NT])
        b_hso = bufs("hso", 4)
        Xr = [sbg(f"Xr{i}", [128, 3, 8], F32R) for i in range(3)]
        b_Xr = bufs("Xr", 3)
        rows = [sbg(f"rows{i}", [8, 3, 128], F32R) for i in range(3)]
        b_rows = bufs("rows", 3)
        ktok4 = sbg("ktok4", [128, 4, 128])
        VK4 = sbg("VK4", [128, 4, 2, 128], F32R)
        Ks4 = sbg("Ks4", [128, 4, 128], F32R)
        ATQ4 = sbg("ATQ4", [128, 4, 256], F32R)
        PPall = sbg("PPall", [128, 2, 2, 2, 128], F32R)
        TTm4 = sbg("TTm4", [128, 4, 128], F32R)
        UW4 = sbg("UW4", [128, 4, 2, 128], F32R)
        GT4 = sbg("GT4", [128, 4, 128], F32R)
        AWn4 = sbg("AWn4", [128, 4, 128], F32R)
        b_ktok4, b_Kbg4, b_Ks4, b_Vb4 = Buf("ktok4"), Buf("Kbg4"), Buf("Ks4"), Buf("Vb4")
        b_ATQ4 = bufs("ATQ4", 2)
        b_PP, b_TTp = bufs("PP", 2), bufs("TTp", 2)
        b_UW, b_GT4, b_AWn4 = bufs("UW", 2), bufs("GT4", 4), Buf("AWn4")
        tmpO = [sbg(f"tmpO{i}", [128, 128]) for i in range(2)]
        b_tmpO = bufs("tmpO", 2)
        Sst = [sbg(f"S{i}", [128, 128], F32R) for i in range(2)]
        b_S = bufs("S", 2)
        ob4 = [sbg(f"ob4{i}", [128, 4, 128], BF16) for i in range(2)]
        b_ob4 = bufs("ob4", 2)
        junk2 = sbg("junk2", [128, 128])
        b_junk2 = Buf("junk2")

        bank = [Buf(f"bank{i}", excl=True) for i in range(8)]
        b_pacc = [bank[0], bank[1]]
        b_pz = bank[2]
        b_p3 = bank[3]

        conv_state = {"open": False}
        cv = {}
        p3 = {}
        b_fn = Buf("fnwb")
        b_wo = bufs("wo", 4)

        def open_conv():
            fw.barrier()
            if dbg:
                dq = sbg("dq", [128, 8, 128])
                b_dq = Buf("dq")
                fw.dma("sp", "dbg0", dbg_d[:, 0:2, :], kqT[:].bitcast(F32), reads=[])
                fw.dma("sp", "dbg1", dbg_d[:, 2, :], vT[:].bitcast(F32), reads=[])
                fw.dma("sp", "dbg2", dbg_d[:, 3, 0:512], Ob4[1][:].rearrange("p t n -> p (t n)"), reads=[])
                fw.op("dve", lambda h: h.tensor_copy(out=dq[:, 0, :], in_=g3[:].rearrange("p h t -> p (h t)")), writes=[b_dq])
                fw.op("dve", lambda h: h.tensor_copy(out=dq[:, 1, :], in_=gc[:].rearrange("p h t -> p (h t)")), writes=[b_dq])
                fw.op("dve", lambda h: h.tensor_copy(out=dq[:, 2, :], in_=lnb[:].rearrange("p h t -> p (h t)")), writes=[b_dq])
                fw.op("dve", lambda h: h.tensor_copy(out=dq[:, 3, 0:96], in_=hs[:, 0:6, :].rearrange("p h t -> p (h t)")), writes=[b_dq])
                fw.op("dve", lambda h: h.tensor_copy(out=dq[:, 4, 0:64], in_=hs[:, 6:10, :].rearrange("p h t -> p (h t)")), writes=[b_dq])
                fw.op("dve", lambda h: h.tensor_copy(out=dq[:, 5, :], in_=ATQ4[:, 3, 0:128].bitcast(F32)), writes=[b_dq])
                fw.op("dve", lambda h: h.tensor_copy(out=dq[:, 6, :], in_=TTm4[:, 3, :].bitcast(F32)), writes=[b_dq])
                fw.dma("sp", "dbg5", dbg_d[:, 6, 0:8 * 128], dq[:].rearrange("p t n -> p (t n)"), reads=[b_dq])
                fw.barrier()
            esG.close()
            conv_state["open"] = True
            esC = ExitStack()
            conv_state["es"] = esC

            def sbc(name, shape, dt=F32):
                return esC.enter_context(nc.sbuf_tensor("s_" + name, list(shape), dt))
            cv["Csb"] = [sbc(f"Csb{i}", [128, 512]) for i in range(2)]
            cv["xp3"] = sbc("xp3", [128, 514])
            cv["acc3"] = [sbc(f"acc3{i}", [128, 512]) for i in range(2)]
            cv["szc"] = [sbc(f"szc{i}", [128, 512]) for i in range(2)]
            cv["Bsb"] = [sbc(f"Bsb{i}", [128, 512]) for i in range(2)]
            p3["wo"] = sbc("wo", [128, 16, D], BF16)
            p3["fnwb"] = sbc("fnwb", [128, D])
            p3["xr"] = [sbc(f"xr{i}", [128, D]) for i in range(3)]
            p3["rr"] = [sbc(f"rr{i}", [128, D]) for i in range(2)]
            p3["junk3"] = sbc("junk3", [128, D], BF16)
            p3["yt"] = [sbc(f"yt{i}", [128, D]) for i in range(3)]
            p3["st3"] = sbc("st3", [128, NT, 4])
            fw.dma("sp", "c4", p3["fnwb"][:], fnw_d.partition_broadcast(128), writes=[b_fn])
            wov = wout_d.rearrange("(g p) n -> p g n", p=128)
            for q4 in range(4):
                fw.dma("pool", f"wo{q4}", p3["wo"][:, q4 * 4:(q4 + 1) * 4, :], wov[:, q4 * 4:(q4 + 1) * 4, :], writes=[b_wo[q4]])

        b_Csb = bufs("Csb", 2)
        b_xp3 = Buf("xp3")
        b_acc3 = bufs("acc3", 2)
        b_szc = bufs("szc", 2)
        b_Bsb = bufs("Bsb", 2)

        def load_w(sg, i):
            fw.dma("pool", f"w{i}", wbuf[i][:], wsg_d[sg].rearrange("(c p) n -> p c n", p=128), writes=[b_wbuf[i]])

        n_sg = 16
        active = [h for h in range(n_heads)] + [8 + c for c in range(n_conv)]
        if active:
            load_w(active[0], 0)

        for t in range(NT):
            for c in range(8):
                fw.op("pe", lambda h, t=t, c=c: h.matmul(PB[2][:, t * 16:(t + 1) * 16], lhsT=hT[:, c, t * 128:(t + 1) * 128],
                                                         rhs=wsm[:, c, :], start=(c == 0), stop=(c == 7)),
                      reads=[b_hT[t], b_const], writes=[b_pz])
        fw.op("act", lambda h: h.copy(out=bg[:].rearrange("p t n -> p (t n)"), in_=PB[2][:, 0:256]), reads=[b_pz], writes=[b_stat])
        bg_b = bg[:, :, 0:8].rearrange("p t h -> p h t")
        bg_a = bg[:, :, 8:16].rearrange("p t h -> p h t")
        fw.op("act", lambda h: h.activation(out=tmpa[:], in_=bg_b, func=AF.Exp, scale=-1.0), reads=[b_stat], writes=[b_stat])
        fw.op("act", lambda h: h.activation(out=lnb[:], in_=tmpa[:], func=AF.Ln, bias=1.0), reads=[b_stat], writes=[b_stat])
        fw.op("dve", lambda h: h.tensor_scalar(out=lnb[:], in0=lnb[:], scalar1=-1.0, scalar2=None, op0=ALU.mult),
              reads=[b_stat], writes=[b_stat])
        for hd in range(8):
            fw.op("dve", lambda h, hd=hd: h.tensor_scalar(out=tmpb[:, hd, :], in0=bg_a[:, hd, :], scalar1=dtb[:, hd:hd + 1],
                                                          scalar2=None, op0=ALU.add),
                  reads=[b_stat, b_const], writes=[b_stat])
        fw.op("act", lambda h: h.activation(out=tmpb[:], in_=tmpb[:], func=AF.Exp), reads=[b_stat], writes=[b_stat])
        fw.op("act", lambda h: h.activation(out=tmpb[:], in_=tmpb[:], func=AF.Ln, bias=1.0), reads=[b_stat], writes=[b_stat])
        for hd in range(8):
            fw.op("dve", lambda h, hd=hd: h.tensor_scalar(out=g3[:, hd, :], in0=tmpb[:, hd, :], scalar1=nexpA[:, hd:hd + 1],
                                                          scalar2=None, op0=ALU.mult),
                  reads=[b_stat, b_small], writes=[b_stat])
        g3f = g3[:].rearrange("p h t -> p (h t)")
        fw.op("pe", lambda h: h.matmul(PB[3][:, 0:128], lhsT=ltri[:], rhs=g3f, start=True, stop=True),
              reads=[b_stat, b_const], writes=[b_p3])
        fw.op("pe", lambda h: h.matmul(PB[3][:, 128:256], lhsT=ones[:], rhs=g3f, start=True, stop=True),
              reads=[b_stat, b_small], writes=[b_p3])
        fw.op("act", lambda h: h.copy(out=gc[:].rearrange("p h t -> p (h t)"), in_=PB[3][:, 0:128]), reads=[b_p3], writes=[b_stat])
        fw.op("act", lambda h: h.copy(out=glb[:].rearrange("p h t -> p (h t)"), in_=PB[3][:, 128:256]), reads=[b_p3], writes=[b_stat])
        fw.op("act", lambda h: h.activation(out=eglb[:], in_=glb[:], func=AF.Exp), reads=[b_stat], writes=[b_stat])
        fw.op("dve", lambda h: h.tensor_tensor(out=base1[:], in0=gc[:], in1=lnb[:], op=ALU.add), reads=[b_stat], writes=[b_stat])

        cnt = {"pacc": 0, "acc": 0, "gb": 0}

        def next_bank():
            pi = cnt["pacc"] % 2
            cnt["pacc"] += 1
            return pi

        def proj_fm(i_w, col0, tb, consume):
            pi = next_bank()
            for c in range(8):
                fw.op("pe", lambda h, c=c, pi=pi: h.matmul(PB[pi][:, :], lhsT=wbuf[i_w][:, c, col0:col0 + 128],
                                                           rhs=hT[:, c, tb * 512:(tb + 1) * 512],
                                                           start=(c == 0), stop=(c == 7)),
                      reads=[b_wbuf[i_w]] + b_hT[tb * 4:tb * 4 + 4], writes=[b_pacc[pi]])
            consume(pi)

        LNQ = float(np.log(128.0 ** -0.5))

        PCH_S4, PCH_S6, PCH_OTHER = 4, 4, 6
        pch = [4]

        def P_unit(hd, iw, tb, kb):
            t0 = tb * 4
            bsl = slice(tb * 512, (tb + 1) * 512)
            x = kb % 3
            dsts = [(kqT[:, 1, bsl], b_qT), (kqT[:, 0, bsl], b_kT), (vT[:, bsl], b_vT)]
            npe = [0]

            def tick():
                npe[0] += 1
                if npe[0] >= pch[0]:
                    npe[0] = 0
                    return True
                return False
            for s in range(3):
                if tb == 0:
                    fw.op("pool", lambda h, s=s: h.memset(xp[s][:, 0:3].bitcast(F32), 0.0), writes=[b_xp[s]])
                g = s * 8 + hd
                for j in range(4):
                    fw.op("pool", lambda h, j=j, g=g: h.tensor_scalar(out=dg[:, j, :], in0=ident[:], scalar1=cqw[:, j * 24 + g:j * 24 + g + 1],
                                                                    scalar2=1.0, op0=ALU.mult, op1=ALU.mult), reads=[b_const], writes=[b_dg])
                pa = next_bank()
                for c in range(8):
                    fw.op("pe", lambda h, c=c, pa=pa, s=s: h.matmul(PB[pa][:, :], lhsT=wbuf[iw][:, c, s * 128:(s + 1) * 128],
                                                                   rhs=hT[:, c, bsl], start=(c == 0), stop=(c == 7)),
                          reads=[b_wbuf[iw]] + b_hT[t0:t0 + 4], writes=[b_pacc[pa]])
                    if tick():
                        yield
                fw.op("act", lambda h, s=s, pa=pa: h.copy(out=xp[s][:, 3:515], in_=PB[pa][:, :]), reads=[b_pacc[pa]], writes=[b_xp[s]])
                yield
                pb = next_bank()
                for j in range(4):
                    fw.op("pe", lambda h, s=s, j=j, pb=pb: h.matmul(PB[pb][:, :], lhsT=dg[:, j, :], rhs=xp[s][:, j:j + 512],
                                                                   start=(j == 0), stop=(j == 3)),
                          reads=[b_dg, b_xp[s]], writes=[b_pacc[pb]])
                    if tick():
                        yield
                fw.op("dve", lambda h, s=s, pb=pb: h.tensor_copy(out=acc[s][:], in_=PB[pb][:, :]), reads=[b_pacc[pb]], writes=[b_acc[s]])
                fw.op("pool", lambda h, s=s: h.tensor_copy(out=xp[s][:, 0:3], in_=xp[s][:, 512:515].bitcast(F32)),
                      reads=[b_xp[s]], writes=[b_xp[s]])
            pz = next_bank()
            for tt in range(4):
                t = t0 + tt
                for c in range(8):
                    fw.op("pe", lambda h, t=t, tt=tt, c=c: h.matmul(PB[pz][:, tt * 128:(tt + 1) * 128],
                                                                   lhsT=hT[:, c, t * 128:(t + 1) * 128],
                                                                   rhs=wbuf[iw][:, c, 384:512], start=(c == 0), stop=(c == 7)),
                          reads=[b_hT[t], b_wbuf[iw]], writes=[b_pacc[pz]])
                    if tick():
                        yield
            for s in range(3):
                o_ap, bl = dsts[s]
                fw.op("act", lambda h, s=s, o_ap=o_ap: h.activation(out=o_ap, in_=acc[s][:], func=AF.Silu),
                      reads=[b_acc[s]], writes=[bl[tb]])
            zview = zs[:, t0:t0 + 4, :]
            fw.op("act", lambda h: h.activation(out=zview.rearrange("p a n -> p (a n)"), in_=PB[pz][:, :], func=AF.Silu),
                  reads=[b_pacc[pz]], writes=[b_zs[tb]])
            fw.op("pool", lambda h: h.tensor_tensor(out=zview, in0=zview, in1=gnwb4[:], op=ALU.mult),
                  reads=[b_zs[tb], b_const], writes=[b_zs[tb]])
            yield
            pq = next_bank()
            for s, bl in ((0, b_kT), (1, b_qT)):
                xs = 1 - s
                fw.op("act", lambda h, s=s, xs=xs: h.activation(out=xp[xs][:, 3:515], in_=kqT[:, s, bsl].bitcast(F32), func=AF.Square),
                      reads=[bl[tb]], writes=[b_xp[xs]])
            for _ in range(3):
                yield
            for s, bl in ((0, b_kT), (1, b_qT)):
                for tt in range(4):
                    c0 = (s * 4 + tt) * 2
                    fw.op("pe", lambda h, s=s, tt=tt, c0=c0: h.matmul(PB[pq][:, c0:c0 + 2], lhsT=xp[1 - s][:, 3 + tt * 128:3 + (tt + 1) * 128],
                                                                     rhs=onesr[:], start=True, stop=True),
                          reads=[b_xp[1 - s], b_small], writes=[b_pacc[pq]])
                yield
            bh = b_hs[tb]
            hsl = slice(t0, t0 + 4)
            fw.op("act", lambda h: h.activation(out=hs[:, 0:2, hsl],
                                                in_=PB[pq][:, 0:16].rearrange("p (s t two) -> p s t two", s=2, t=4, two=2)[:, :, :, 0],
                                                func=AF.Ln, bias=EPS), reads=[b_pacc[pq]], writes=[bh])
            fw.op("dve", lambda h: h.tensor_scalar(out=hs[:, 0, hsl], in0=hs[:, 0, hsl], scalar1=-0.5, scalar2=None, op0=ALU.mult),
                  reads=[bh], writes=[bh])
            fw.op("dve", lambda h: h.tensor_scalar(out=hs[:, 1, hsl], in0=hs[:, 1, hsl], scalar1=-0.5, scalar2=LNQ,
                                                   op0=ALU.mult, op1=ALU.add), reads=[bh], writes=[bh])
            fw.op("dve", lambda h: h.tensor_tensor(out=hs[:, 2, hsl], in0=base1[:, hd, hsl], in1=hs[:, 0, hsl], op=ALU.add),
                  reads=[bh, b_stat], writes=[bh])
            fw.op("dve", lambda h: h.tensor_tensor(out=hs[:, 3, hsl], in0=gc[:, hd, hsl], in1=hs[:, 1, hsl], op=ALU.add),
                  reads=[bh, b_stat], writes=[bh])
            fw.op("dve", lambda h: h.tensor_tensor(out=hs[:, 4, hsl], in0=hs[:, 0, hsl], in1=gc[:, hd, hsl], op=ALU.subtract),
                  reads=[bh, b_stat], writes=[bh])
            fw.op("dve", lambda h: h.tensor_tensor(out=hs[:, 5, hsl], in0=hs[:, 4, hsl], in1=glb[:, hd, hsl], op=ALU.add),
                  reads=[bh, b_stat], writes=[bh])
            fw.op("act", lambda h: h.activation(out=hs[:, 6, hsl], in_=hs[:, 2, hsl], func=AF.Exp), reads=[bh], writes=[bh])
            fw.op("dve", lambda h: h.tensor_scalar(out=hs[:, 10, hsl], in0=hs[:, 6, hsl], scalar1=-1.0, scalar2=None, op0=ALU.mult),
                  reads=[bh], writes=[bh])
            fw.op("act", lambda h: h.activation(out=hs[:, 7, hsl], in_=hs[:, 5, hsl], func=AF.Exp), reads=[bh], writes=[bh])
            fw.op("act", lambda h: h.activation(out=hs[:, 8, hsl], in_=lnb[:, hd, hsl], func=AF.Exp), reads=[bh, b_stat], writes=[bh])
            fw.op("act", lambda h: h.activation(out=hs[:, 9, hsl], in_=hs[:, 3, hsl], func=AF.Exp), reads=[bh], writes=[bh])
            fw.op("act", lambda h: h.copy(out=Xr[x][:, :, 0:4], in_=hs[:, 2:5, hsl]), reads=[bh], writes=[b_Xr[x]])
            fw.op("dve", lambda h: h.tensor_tensor(out=Xr[x][:, :, 4:8], in0=hs[:, 2:5, hsl], in1=Xr[x][:, :, 0:4].bitcast(F32),
                                                   op=ALU.subtract), reads=[bh, b_Xr[x]], writes=[b_Xr[x]])
            for _ in range(8):
                yield
            yield "TAIL"
            pr = next_bank()
            for k3 in range(3):
                fw.op("pe", lambda h, k3=k3: h.transpose(out=PB[pr][0:8, k3 * 128:(k3 + 1) * 128], in_=Xr[x][:, k3, :].bitcast(F32),
                                                         identity=ident[:]), reads=[b_Xr[x], b_const], writes=[b_pacc[pr]])
            fw.op("act", lambda h: h.copy(out=rows[x][:].rearrange("p a n -> p (a n)"), in_=PB[pr][0:8, 0:384]),
                  reads=[b_pacc[pr]], writes=[b_rows[x]])
            yield

        pending = []

        def G_unit(hd, tg, kb):
            t0 = tg * 4
            pch[0] = PCH_OTHER
            x = kb % 3
            oi = cnt["gb"] % 2
            cnt["gb"] += 1
            bh = b_hs[tg]
            if tg == 0:
                fw.op("pool", lambda h: h.memset(Sst[0][:].bitcast(F32), 0.0), writes=[b_S[0]])
            for j in range(4):
                tsl = slice((t0 + j) * 128, (t0 + j + 1) * 128)
                fw.op("pe", lambda h, j=j, tsl=tsl: h.transpose(out=PB[2][:, j * 128:(j + 1) * 128], in_=kqT[:, 0, tsl].bitcast(F32),
                                                                identity=ident[:]), reads=[b_kT[tg], b_const], writes=[bank[2]])
            for j in range(4):
                tsl = slice((t0 + j) * 128, (t0 + j + 1) * 128)
                fw.op("pe", lambda h, j=j, tsl=tsl: h.transpose(out=PB[3][:, j * 128:(j + 1) * 128], in_=vT[:, tsl].bitcast(F32),
                                                                identity=ident[:]), reads=[b_vT[tg], b_const], writes=[bank[3]])
            yield
            fw.op("act", lambda h: h.copy(out=ktok4[:].rearrange("p a n -> p (a n)"), in_=PB[2][:, :]), reads=[bank[2]], writes=[b_ktok4])

            def bc(row):
                return hs[:, row, t0:t0 + 4].unsqueeze(2).to_broadcast([128, 4, 128])
            fw.op("dve", lambda h: h.tensor_tensor(out=VK4[:, :, 0, :], in0=PB[3][:, :].rearrange("p (a n) -> p a n", a=4), in1=bc(8),
                                                   op=ALU.mult), reads=[bank[3], bh], writes=[b_Vb4])
            fw.op("pool", lambda h: h.tensor_tensor(out=VK4[:, :, 1, :], in0=ktok4[:], in1=bc(10), op=ALU.mult),
                  reads=[b_ktok4, bh], writes=[b_Kbg4])
            fw.op("pool", lambda h: h.tensor_tensor(out=Ks4[:], in0=ktok4[:], in1=bc(7), op=ALU.mult),
                  reads=[b_ktok4, bh], writes=[b_Ks4])
            for half in range(2):
                for jj in range(2):
                    j = half * 2 + jj
                    tsl = slice((t0 + j) * 128, (t0 + j + 1) * 128)
                    fw.op("pe", lambda h, half=half, jj=jj, tsl=tsl: h.matmul(
                        PB[4 + half][:, jj * 256:(jj + 1) * 256].rearrange("p (a n) -> p a n", a=2),
                        lhsT=kqT[:, 0, tsl], rhs=kqT[:, :, tsl], start=True, stop=True),
                        reads=[b_kT[tg], b_qT[tg]], writes=[bank[4 + half]])
                for jj in range(2):
                    j = half * 2 + jj
                    osl = slice(jj * 256, (jj + 1) * 256)
                    fw.op("pe", lambda h, half=half, osl=osl: h.matmul(PB[6 + half][:, osl], lhsT=identb[:], rhs=maskb[:],
                                                                      start=True, stop=False),
                          reads=[b_small, b_const], writes=[bank[6 + half]])
                    fw.op("pe", lambda h, half=half, osl=osl, j=j: h.matmul(
                        PB[6 + half][:, osl], lhsT=selr[:, j:j + 1].to_broadcast([8, 128]),
                        rhs=rows[x][:, 0:2, :].rearrange("p a n -> p (a n)"), start=False, stop=False),
                        reads=[b_small, b_rows[x]], writes=[bank[6 + half]])
                    fw.op("pe", lambda h, half=half, osl=osl, j=j: h.matmul(
                        PB[6 + half][:, osl], lhsT=rows[x][:, 2, :], rhs=selr[:, j:j + 1].to_broadcast([8, 256]),
                        start=False, stop=True),
                        reads=[b_small, b_rows[x]], writes=[bank[6 + half]])
                yield
            for half in range(2):
                asl = ATQ4[:, half * 2:half * 2 + 2, :].rearrange("p a n -> p (a n)")
                fw.op("act", lambda h, half=half, asl=asl: h.activation(out=asl, in_=PB[6 + half][:, :], func=AF.Exp),
                      reads=[bank[6 + half]], writes=[b_ATQ4[half]])
                fw.op("dve", lambda h, half=half, asl=asl: h.tensor_tensor(out=asl, in0=PB[4 + half][:, :], in1=asl.bitcast(F32),
                                                                           op=ALU.mult),
                      reads=[bank[4 + half], b_ATQ4[half]], writes=[b_ATQ4[half]])
            for j in range(4):
                fw.op("pe", lambda h, j=j: h.transpose(out=PB[2][:, j * 128:(j + 1) * 128], in_=ATQ4[:, j, 0:128].bitcast(F32),
                                                       identity=ident[:]), reads=[b_ATQ4[j // 2], b_const], writes=[bank[2]])
            yield
            fw.op("act", lambda h: h.copy(out=PPall[:, :, 0, :, :], in_=PB[2][:, :].rearrange("p (q t n) -> p q t n", q=2, t=2)),
                  reads=[bank[2]], writes=b_PP)
            fw.op("pool", lambda h: h.tensor_tensor(out=TTm4[:], in0=ident4[:], in1=ATQ4[:, :, 0:128].bitcast(F32), op=ALU.subtract),
                  reads=b_ATQ4 + [b_const], writes=b_TTp)
            def sq(q, m):
                for t in range(2):
                    j = 2 * q + t
                    ptp = ATQ4[:, j, 0:128] if m == 1 else PPall[:, q, 1, t, :]
                    rd = [b_PP[q]] + ([b_ATQ4[q]] if m == 1 else [])
                    fw.op("pe", lambda h, q=q, t=t, ptp=ptp: h.matmul(PB[3 + q][:, t * 128:(t + 1) * 128], lhsT=ptp,
                                                                     rhs=PPall[:, q, 0, t, :], start=True, stop=True),
                          reads=rd, writes=[bank[3 + q]])
                    if m < 6:
                        fw.op("pe", lambda h, q=q, t=t, ptp=ptp: h.matmul(PB[3 + q][:, 256 + t * 128:256 + (t + 1) * 128],
                                                                         lhsT=PPall[:, q, 0, t, :], rhs=ptp, start=True, stop=True),
                              reads=rd, writes=[bank[3 + q]])
                if m < 6:
                    fw.op("act", lambda h, q=q: h.copy(out=PPall[:, q, :, :, :].rearrange("p a t n -> p (a t n)"), in_=PB[3 + q][:, :]),
                          reads=[bank[3 + q]], writes=[b_PP[q]])
                else:
                    fw.op("act", lambda h, q=q: h.copy(out=PPall[:, q, 0, :, :].rearrange("p t n -> p (t n)"), in_=PB[3 + q][:, 0:256]),
                          reads=[bank[3 + q]], writes=[b_PP[q]])

            def prod(q):
                for t in range(2):
                    j = 2 * q + t
                    fw.op("pe", lambda h, q=q, t=t, j=j: h.matmul(PB[5 + q][:, t * 128:(t + 1) * 128], lhsT=PPall[:, q, 0, t, :],
                                                                 rhs=TTm4[:, j, :], start=True, stop=True),
                          reads=[b_PP[q], b_TTp[q]], writes=[bank[5 + q]])
                tsl2 = TTm4[:, 2 * q:2 * q + 2, :].rearrange("p a n -> p (a n)")
                fw.op("dve", lambda h, q=q, tsl2=tsl2: h.tensor_tensor(out=tsl2, in0=PB[5 + q][:, 0:256], in1=tsl2.bitcast(F32), op=ALU.add),
                      reads=[bank[5 + q], b_TTp[q]], writes=[b_TTp[q]])

            pch[0] = PCH_S4
            sq(0, 1)
            yield
            sq(1, 1)
            yield
            for m in range(2, 7):
                for q in range(2):
                    prod(q)
                    sq(q, m)
                    yield
                if m == 3:
                    while pending:
                        pending.pop(0)()
            prod(0)
            yield
            prod(1)
            yield
            pch[0] = PCH_OTHER
            for j in range(4):
                fw.op("pe", lambda h, j=j: h.matmul(PB[2 + j // 2][:, (j % 2) * 256:(j % 2 + 1) * 256].rearrange("p (a n) -> p a n", a=2),
                                                    lhsT=TTm4[:, j, :], rhs=VK4[:, j, :, :], start=True, stop=True),
                      reads=[b_TTp[j // 2], b_Vb4, b_Kbg4], writes=[bank[2 + j // 2]])
            yield 2
            fw.op("act", lambda h: h.copy(out=UW4[:, 0:2, :, :].rearrange("p t a n -> p (t a n)"), in_=PB[2][:, :]),
                  reads=[bank[2]], writes=[b_UW[0]])
            fw.op("dve", lambda h: h.tensor_copy(out=UW4[:, 2:4, :, :].rearrange("p t a n -> p (t a n)"), in_=PB[3][:, :]),
                  reads=[bank[3]], writes=[b_UW[1]])
            for j in range(4):
                fw.op("pe", lambda h, j=j: h.matmul(PB[4][:, j * 128:(j + 1) * 128], lhsT=Ks4[:, j, :], rhs=UW4[:, j, 0, :],
                                                    start=True, stop=True), reads=[b_Ks4, b_UW[j // 2]], writes=[bank[4]])
            for j in range(4):
                fw.op("pe", lambda h, j=j: h.matmul(PB[5][:, j * 128:(j + 1) * 128], lhsT=UW4[:, j, 1, :], rhs=Ks4[:, j, :],
                                                    start=True, stop=True), reads=[b_Ks4, b_UW[j // 2]], writes=[bank[5]])
            for j in range(4):
                fw.op("pe", lambda h, j=j: h.matmul(PB[6][:, j * 128:(j + 1) * 128], lhsT=UW4[:, j, 1, :], rhs=ATQ4[:, j, 128:256],
                                                    start=True, stop=True), reads=[b_ATQ4[j // 2], b_UW[j // 2]], writes=[bank[6]])
            yield 2
            for j in range(4):
                t = t0 + j
                fw.op("dve", lambda h, j=j, t=t: h.scalar_tensor_tensor(out=GT4[:, j, :], in0=ident[:], scalar=eglb[:, hd, t:t + 1],
                                                                       in1=PB[5][:, j * 128:(j + 1) * 128], op0=ALU.mult, op1=ALU.add),
                      reads=[bank[5], b_stat, b_const], writes=[b_GT4[j]])
            fw.op("act", lambda h: h.copy(out=ktok4[:].rearrange("p a n -> p (a n)"), in_=PB[4][:, :]), reads=[bank[4]], writes=[b_ktok4])
            fw.op("act", lambda h: h.copy(out=AWn4[:].rearrange("p a n -> p (a n)"), in_=PB[6][:, :]), reads=[bank[6]], writes=[b_AWn4])
            pch[0] = PCH_S6
            defer = []
            for j in range(4):
                t = t0 + j
                p = t % 2
                sc, sn = t % 2, (t + 1) % 2
                tsl = slice(t * 128, (t + 1) * 128)
                fw.op("pe", lambda h, j=j, sc=sc: h.matmul(PB[7][:, 0:128], lhsT=GT4[:, j, :], rhs=Sst[sc][:], start=True, stop=True),
                      reads=[b_GT4[j], b_S[sc]], writes=[bank[7]])
                qb = 2 if p == 0 else 4
                ob = 3 if p == 0 else 5
                fw.op("pe", lambda h, tsl=tsl, sc=sc, qb=qb: h.matmul(PB[qb][:, 0:128], lhsT=kqT[:, 1, tsl], rhs=Sst[sc][:],
                                                                      start=True, stop=True),
                      reads=[b_qT[tg], b_S[sc]], writes=[bank[qb]])
                fw.op("pe", lambda h, j=j, ob=ob: h.matmul(PB[ob][:, 0:128], lhsT=ATQ4[:, j, 128:256], rhs=UW4[:, j, 0, :],
                                                           start=True, stop=False),
                      reads=[b_ATQ4[j // 2], b_UW[j // 2]], writes=[bank[ob]])
                fw.op("pe", lambda h, j=j, ob=ob, sc=sc: h.matmul(PB[ob][:, 0:128], lhsT=AWn4[:, j, :], rhs=Sst[sc][:],
                                                                  start=False, stop=True),
                      reads=[b_AWn4, b_S[sc]], writes=[bank[ob]])
                yield 1
                fw.op("dve", lambda h, j=j, sn=sn: h.tensor_tensor(out=Sst[sn][:], in0=PB[7][:, 0:128], in1=ktok4[:, j, :], op=ALU.add),
                      reads=[bank[7], b_ktok4], writes=[b_S[sn]])
                for f in defer:
                    f()
                defer = []

                def off_chain(j=j, t=t, p=p, qb=qb, ob=ob):
                    fw.op("act", lambda h: h.mul(out=tmpO[p][:], in_=PB[qb][:, 0:128], mul=hs[:, 9, t:t + 1]),
                          reads=[bank[qb], bh], writes=[b_tmpO[p]])
                    fw.op("dve", lambda h: h.tensor_tensor(out=Ob4[oi][:, j, :], in0=PB[ob][:, 0:128], in1=tmpO[p][:],
                                                           op=ALU.add),
                          reads=[bank[ob], b_tmpO[p]], writes=[b_Ob4[oi]])
                    fw.op("act", lambda h: h.activation(out=junk2[:], in_=Ob4[oi][:, j, :], func=AF.Square,
                                                        accum_out=hso[:, 0, t:t + 1]),
                          reads=[b_Ob4[oi]], writes=[b_junk2, b_hso[tg]])
                defer.append(off_chain)
            for f in defer:
                f()
            pch[0] = PCH_OTHER

            fw.op("dve", lambda h: h.tensor_scalar(out=hso[:, 1, t0:t0 + 4], in0=hso[:, 0, t0:t0 + 4], scalar1=1.0 / 128,
                                                   scalar2=EPS, op0=ALU.mult, op1=ALU.add), reads=[b_hso[tg]], writes=[b_hso[tg]])
            fw.op("act", lambda h: h.activation(out=hso[:, 1, t0:t0 + 4], in_=hso[:, 1, t0:t0 + 4], func=AF.Ln),
                  reads=[b_hso[tg]], writes=[b_hso[tg]])
            fw.op("act", lambda h: h.activation(out=hso[:, 1, t0:t0 + 4], in_=hso[:, 1, t0:t0 + 4], func=AF.Exp, scale=-0.5),
                  reads=[b_hso[tg]], writes=[b_hso[tg]])
            for j in range(4):
                t = t0 + j
                fw.op("dve", lambda h, j=j, t=t: h.scalar_tensor_tensor(out=ob4[oi][:, j, :], in0=Ob4[oi][:, j, :],
                                                                       scalar=hso[:, 1, t:t + 1], in1=zs[:, t, :],
                                                                       op0=ALU.mult, op1=ALU.mult),
                      reads=[b_Ob4[oi], b_hso[tg], b_zs[tg]], writes=[b_ob4[oi]])

            def out_stage():
                pov = PB[7][:, 256:512].bitcast(BF16)
                for j in range(4):
                    fw.op("pe", lambda h, j=j: h.transpose(out=pov[:, j * 128:(j + 1) * 128], in_=ob4[oi][:, j, :], identity=identb[:]),
                          reads=[b_ob4[oi], b_small], writes=[bank[7]])
                fw.op("act", lambda h: h.copy(out=mixT[:, hd, t0 * 128:(t0 + 4) * 128], in_=pov), reads=[bank[7]], writes=[b_mix[hd][tg]])
            pending.append(out_stage)
            yield

        def run_all(gen):
            for _ in gen:
                pass

        stash = []

        def merge(G, P, ratio):
            k = 0
            g_alive, p_alive = True, P is not None
            while g_alive:
                try:
                    next(G)
                except StopIteration:
                    g_alive = False
                k += 1
                if k == 6:
                    while stash:
                        run_all(stash.pop(0))
                if p_alive:
                    try:
                        next(P)
                    except StopIteration:
                        p_alive = False
            while stash:
                run_all(stash.pop(0))
            if p_alive:
                for v in P:
                    if v == "TAIL":
                        stash.append(P)
                        break

        heads = [sg for sg in active if sg < 8]
        convs = [sg for sg in active if sg >= 8]
        blocks = []
        for idx, hd in enumerate(heads):
            for tb in range(4):
                blocks.append((idx, hd, tb))

        def start_P(k):
            idx, hd, tb = blocks[k]
            if tb == 0 and idx + 1 < len(active):
                load_w(active[idx + 1], (idx + 1) % 2)
            return P_unit(hd, idx % 2, tb, k)

        if blocks:
            run_all(start_P(0))
            if len(blocks) > 1:
                run_all(start_P(1))
            for k in range(len(blocks)):
                idx, hd, tb = blocks[k]
                Pn = start_P(k + 2) if k + 2 < len(blocks) else None
                merge(G_unit(hd, tb, k), Pn, 1)
            while stash:
                run_all(stash.pop(0))
            while pending:
                pending.pop(0)()

        for ci, sg in enumerate(convs):
            idx = len(heads) + ci
            iw = idx % 2
            if idx + 1 < len(active):
                load_w(active[idx + 1], (idx + 1) % 2)
            if True:
                c = sg - 8
                if not conv_state["open"]:
                    open_conv()
                    Csb, xp3, acc3, szc, Bsb = cv["Csb"], cv["xp3"], cv["acc3"], cv["szc"], cv["Bsb"]
                fw.op("pool", lambda h: h.memset(xp3[:, 0:2], 0.0), writes=[b_xp3])
                for tb in range(4):
                    st = {}

                    def cons_C(pi, st=st):
                        ci_ = cnt["acc"] % 2
                        st["ci"] = ci_
                        fw.op("act", lambda h: h.copy(out=Csb[ci_][:], in_=PB[pi][:, :]), reads=[b_pacc[pi]], writes=[b_Csb[ci_]])

                    def cons_h(pi, st=st, c=c):
                        ci_ = st["ci"]
                        ai = cnt["acc"] % 2
                        cnt["acc"] += 1
                        st["ai"] = ai
                        fw.op("dve", lambda h: h.tensor_tensor(out=xp3[:, 2:514], in0=PB[pi][:, :], in1=Csb[ci_][:], op=ALU.mult),
                              reads=[b_pacc[pi], b_Csb[ci_]], writes=[b_xp3])
                        fw.op("dve", lambda h: h.tensor_scalar(out=acc3[ai][:], in0=xp3[:, 2:514], scalar1=cw[:, 2 * 8 + c:2 * 8 + c + 1],
                                                               scalar2=cb[:, c:c + 1], op0=ALU.mult, op1=ALU.add),
                              reads=[b_xp3, b_const], writes=[b_acc3[ai]])
                        for j in (1, 0):
                            fw.op("dve", lambda h, j=j: h.scalar_tensor_tensor(out=acc3[ai][:], in0=xp3[:, j:j + 512],
                                                                               scalar=cw[:, j * 8 + c:j * 8 + c + 1],
                                                                               in1=acc3[ai][:], op0=ALU.mult, op1=ALU.add),
                                  reads=[b_xp3, b_const, b_acc3[ai]], writes=[b_acc3[ai]])
                        fw.op("dve", lambda h: h.tensor_copy(out=xp3[:, 0:2], in_=xp3[:, 512:514]), reads=[b_xp3], writes=[b_xp3])

                    def cons_B(pi, st=st):
                        ai = st["ai"]
                        fw.op("act", lambda h: h.copy(out=Bsb[ai][:], in_=PB[pi][:, :]), reads=[b_pacc[pi]], writes=[b_Bsb[ai]])
                        fw.op("pool", lambda h: h.tensor_tensor(out=acc3[ai][:], in0=Bsb[ai][:], in1=acc3[ai][:], op=ALU.mult),
                              reads=[b_Bsb[ai], b_acc3[ai]], writes=[b_acc3[ai]])

                    def cons_z(pi, st=st, c=c, tb=tb):
                        ai = st["ai"]
                        fw.op("act", lambda h: h.activation(out=szc[ai][:], in_=PB[pi][:, :], func=AF.Silu),
                              reads=[b_pacc[pi]], writes=[b_szc[ai]])
                        fw.op("pool", lambda h: h.tensor_tensor(out=mixT[:, 8 + c, tb * 512:(tb + 1) * 512], in0=acc3[ai][:],
                                                                in1=szc[ai][:], op=ALU.mult),
                              reads=[b_acc3[ai], b_szc[ai]], writes=[b_mix[8 + c][tb]])

                    proj_fm(iw, 128, tb, cons_C)
                    proj_fm(iw, 256, tb, cons_h)
                    proj_fm(iw, 0, tb, cons_B)
                    proj_fm(iw, 384, tb, cons_z)

        if not conv_state["open"]:
            open_conv()
        if dbg:
            fw.barrier()
            stg = conv_state["es"].enter_context(nc.sbuf_tensor("s_dbgstg", [128, 2048], F32))
            for slot, src in ((5, mixT[:, 0, :]), (7, mixT[:, 8, :])):
                fw.barrier()
                fw.op("dve", lambda h, src=src: h.tensor_copy(out=stg[:], in_=src))
                fw.barrier()
                fw.dma("sp", f"dbgs{slot}", dbg_d[:, slot, :], stg[:], reads=[])
            fw.barrier()
        if True:
            wo, fnwb, xr, rr, junk3, yt, st3 = p3["wo"], p3["fnwb"], p3["xr"], p3["rr"], p3["junk3"], p3["yt"], p3["st3"]
            b_xr = bufs("xr", 3)
            b_rr = bufs("rr", 2)
            b_junk3 = Buf("junk3")
            b_yt = bufs("yt", 3)
            b_st3 = bufs("st3", NT)
            last_tok = None
            for t in range(NT):
                i = t % 2
                i3 = t % 3
                tsl = slice(t * 128, (t + 1) * 128)
                if t == 0:
                    for tt in range(2):
                        fw.dma("sp", f"xr{tt % 3}", xr[tt % 3][:], x_d[tt * 128:(tt + 1) * 128, :], writes=[b_xr[tt % 3]])
                if t + 2 < NT:
                    tn = t + 2
                    fw.dma("sp", f"xr{tn % 3}", xr[tn % 3][:], x_d[tn * 128:(tn + 1) * 128, :], writes=[b_xr[tn % 3]])
                for half in range(2):
                    for g in range(16):
                        fw.op("pe", lambda h, g=g, half=half, tsl=tsl: h.matmul(PB[half][:, :], lhsT=mixT[:, g, tsl],
                                                                               rhs=wo[:, g, half * 512:(half + 1) * 512],
                                                                               start=(g == 0), stop=(g == 15)),
                              reads=[b_mix[g][t // 4], b_wo[g // 4]], writes=[b_pacc[half]])
                    fw.op("dve", lambda h, i=i, i3=i3, half=half: h.tensor_tensor(out=rr[i][:, half * 512:(half + 1) * 512],
                                                                           in0=PB[half][:, :],
                                                                           in1=xr[i3][:, half * 512:(half + 1) * 512], op=ALU.add),
                          reads=[b_pacc[half], b_xr[i3]], writes=[b_rr[i]])
                fw.op("act", lambda h, i=i, t=t: h.activation(out=junk3[:], in_=rr[i][:], func=AF.Square, accum_out=st3[:, t, 0:1]),
                      reads=[b_rr[i]], writes=[b_junk3, b_st3[t]])
                fw.op("dve", lambda h, t=t: h.tensor_scalar(out=st3[:, t, 1:2], in0=st3[:, t, 0:1], scalar1=1.0 / D, scalar2=EPS,
                                                            op0=ALU.mult, op1=ALU.add), reads=[b_st3[t]], writes=[b_st3[t]])
                fw.op("act", lambda h, t=t: h.activation(out=st3[:, t, 2:3], in_=st3[:, t, 1:2], func=AF.Ln),
                      reads=[b_st3[t]], writes=[b_st3[t]])
                fw.op("act", lambda h, t=t: h.activation(out=st3[:, t, 3:4], in_=st3[:, t, 2:3], func=AF.Exp, scale=-0.5),
                      reads=[b_st3[t]], writes=[b_st3[t]])
                fw.op("dve", lambda h, i=i, i3=i3, t=t: h.scalar_tensor_tensor(out=yt[i3][:], in0=rr[i][:], scalar=st3[:, t, 3:4],
                                                                         in1=fnwb[:], op0=ALU.mult, op1=ALU.mult),
                      reads=[b_rr[i], b_st3[t], b_fn], writes=[b_yt[i3]])
                last_tok = fw.dma("sp", f"y{i3}", y_d[tsl, :], yt[i3][:], reads=[b_yt[i3]])
            fw.barrier()
        conv_state["es"].close()
        es2.close()
        fw.emit()
    return nc


_NC_CACHE = {}


def _host_layout(inputs):
    f = np.float32
    w_in = np.asarray(inputs["w_in"][0], dtype=f)
    wsg = np.empty((16, D, 512), dtype=f)
    for hd in range(8):
        for s in range(4):
            wsg[hd, :, s * 128:(s + 1) * 128] = w_in[:, s * 1024 + hd * 128: s * 1024 + (hd + 1) * 128]
    for c in range(8):
        for s in range(4):
            base = 4112 + s * 1024
            wsg[8 + c, :, s * 128:(s + 1) * 128] = w_in[:, base + c * 128: base + (c + 1) * 128]
    wsm = np.ascontiguousarray(w_in[:, 4096:4112])
    cqw = np.ascontiguousarray(np.asarray(inputs["conv_qkv_w"][0], dtype=f).reshape(4, 24, 128).transpose(2, 0, 1)).reshape(128, 96)
    cw = np.ascontiguousarray(np.asarray(inputs["conv_w"][0], dtype=f).reshape(3, 8, 128).transpose(2, 0, 1)).reshape(128, 24)
    cb = np.ascontiguousarray(np.asarray(inputs["conv_b"][0], dtype=f).reshape(8, 128).T)
    i = np.arange(128)
    ident = np.eye(128, dtype=f)
    ltri = (i[:, None] <= i[None, :]).astype(f)
    maskS = np.where(i[None, :] > i[:, None], 0.0, NEG).astype(f)
    maskI = np.where(i[None, :] >= i[:, None], 0.0, NEG).astype(f)
    mask = np.concatenate([maskS, maskI], axis=1)
    sel = np.zeros((8, 4), dtype=f)
    sel[np.arange(4), np.arange(4)] = 1.0
    sel[4 + np.arange(4), np.arange(4)] = 1.0
    common = dict(
        wsg=wsg, wsm=wsm, wout=np.ascontiguousarray(inputs["w_out"][0], dtype=f),
        cqw=cqw, cw=cw, cb=cb,
        nw=np.asarray(inputs["norm_in_w"], dtype=f).reshape(1, D),
        fnw=np.asarray(inputs["final_norm_w"], dtype=f).reshape(1, D),
        gnw=np.asarray(inputs["gdn_norm_w"], dtype=f).reshape(1, 128),
        alog=np.asarray(inputs["A_log"], dtype=f).reshape(1, 8),
        dtb=np.asarray(inputs["dt_bias"], dtype=f).reshape(1, 8),
        ident=ident, ltri=ltri, mask=mask, sel=sel,
    )
    return common


def kernel(**inputs):
    x = np.asarray(inputs["x"], dtype=np.float32)
    common = _host_layout(inputs)
    if "nc" not in _NC_CACHE:
        _NC_CACHE["nc"] = build_nc()
    nc = _NC_CACHE["nc"]
    in_maps = [dict(common, x=np.ascontiguousarray(x[b])) for b in range(8)]
    res = run_bass_kernel_spmd(nc, in_maps, core_ids=list(range(8)))
    return np.stack([np.asarray(r["y"], dtype=np.float32) for r in res.results], axis=0)
```
